# Optimizing a Trainium2 kernel written in Bass

```python
import jax, jax.numpy as jnp
from jax import lax
import numpy as np

D_MODEL = 1024
BATCH = 4
SEQ = 4096
DEPTH = 1

MIX_WIDTH = D_MODEL
CONV_WIDTH = MIX_WIDTH // 2
CONV_GROUPS = 8
CONV_KERNEL = 31
GM_WIDTH = MIX_WIDTH - CONV_WIDTH
GM_HEADS = 8
GM_HEAD_DIM = GM_WIDTH // GM_HEADS
CHUNK = 128
IN_COLS = 2 * CONV_WIDTH + 2 * GM_WIDTH
MEM_LEN = 256
XA_HEADS = 4
XA_HEAD_DIM = D_MODEL // XA_HEADS
FFN_HIDDEN = ((8 * D_MODEL // 3 + 255) // 256) * 256
RMS_EPS = 1e-6
LN_EPS = 1e-5

kernel_name = "hybrid_conv_gmlp_xattn_block"


def rmsnorm(x, g):
    xf = x.astype(jnp.float32)
    y = xf * lax.rsqrt(jnp.mean(xf * xf, axis=-1, keepdims=True) + RMS_EPS)
    return (y * g.astype(jnp.float32)).astype(x.dtype)


def layernorm(x, g, b):
    xf = x.astype(jnp.float32)
    mu = jnp.mean(xf, axis=-1, keepdims=True)
    var = jnp.mean(jnp.square(xf - mu), axis=-1, keepdims=True)
    y = (xf - mu) * lax.rsqrt(var + LN_EPS)
    return (y * g.astype(jnp.float32) + b.astype(jnp.float32)).astype(x.dtype)


def causal_depthwise_conv(a, w, b):
    k, c = w.shape
    a_pad = jnp.pad(a, ((0, 0), (k - 1, 0), (0, 0)))
    y = lax.conv_general_dilated(
        a_pad, w[:, None, :], window_strides=(1,), padding='VALID',
        dimension_numbers=('NWC', 'WIO', 'NWC'), feature_group_count=c)
    return y + b


def conformer_conv_group(za, zg, conv_w, conv_b, ln_g, ln_b):
    a = za * jax.nn.sigmoid(zg)
    a = causal_depthwise_conv(a, conv_w, conv_b)
    a = layernorm(a, ln_g, ln_b)
    return jax.nn.silu(a)


def gmlp_group(zu, zv, ln_g, ln_b, w_s, b_s):
    u = jax.nn.gelu(zu)
    v = layernorm(jax.nn.gelu(zv), ln_g, ln_b)
    bsz, s, _ = v.shape
    vh = v.reshape(bsz, s // CHUNK, CHUNK, GM_HEADS, GM_HEAD_DIM)
    mask = jnp.tril(jnp.ones((CHUNK, CHUNK), dtype=bool))
    ws = jnp.where(mask[None], w_s, jnp.zeros_like(w_s))
    mixed = jnp.einsum('hts,bnshd->bnthd', ws, vh)
    mixed = mixed + b_s.T[None, None, :, :, None]
    return u * mixed.reshape(bsz, s, GM_WIDTH)


def cross_attention(hn, mn, wq, wkv, wo):
    bsz, s, _ = hn.shape
    q = (hn @ wq).reshape(bsz, s, XA_HEADS, XA_HEAD_DIM)
    kv = mn @ wkv
    k, v = jnp.split(kv, 2, axis=-1)
    k = k.reshape(bsz, MEM_LEN, XA_HEADS, XA_HEAD_DIM)
    v = v.reshape(bsz, MEM_LEN, XA_HEADS, XA_HEAD_DIM)
    scale = XA_HEAD_DIM ** -0.5
    scores = jnp.einsum('bshd,bmhd->bhsm', q, k).astype(jnp.float32) * scale
    p = jax.nn.softmax(scores, axis=-1).astype(v.dtype)
    o = jnp.einsum('bhsm,bmhd->bshd', p, v).reshape(bsz, s, D_MODEL)
    return o @ wo


def swiglu(hn, w_gate_up, w_down):
    gu = hn @ w_gate_up
    g, u = jnp.split(gu, 2, axis=-1)
    return (jax.nn.silu(g) * u) @ w_down


def setup_inputs(seed: int = 0) -> dict:
    key = jax.random.key(seed)
    ks = jax.random.split(key, 24)
    f32 = jnp.float32

    def nrm(k, shape, scale):
        return jax.random.normal(k, shape, f32) * scale

    def gain(k, n):
        return jnp.ones((n,), f32) + 0.05 * jax.random.normal(k, (n,), f32)

    return {
        "x": nrm(ks[0], (BATCH, SEQ, D_MODEL), 1.0),
        "mem": nrm(ks[1], (BATCH, MEM_LEN, D_MODEL), 1.0),
        "norm_mix_g": gain(ks[2], D_MODEL),
        "w_in": nrm(ks[3], (D_MODEL, IN_COLS), D_MODEL ** -0.5),
        "b_in": nrm(ks[4], (IN_COLS,), 0.02),
        "conv_w": nrm(ks[5], (CONV_KERNEL, CONV_WIDTH), CONV_KERNEL ** -0.5),
        "conv_b": nrm(ks[6], (CONV_WIDTH,), 0.02),
        "conv_ln_g": gain(ks[7], CONV_WIDTH),
        "conv_ln_b": nrm(ks[8], (CONV_WIDTH,), 0.02),
        "gm_ln_g": gain(ks[9], GM_WIDTH),
        "gm_ln_b": nrm(ks[10], (GM_WIDTH,), 0.02),
        "gm_w_s": nrm(ks[11], (GM_HEADS, CHUNK, CHUNK), CHUNK ** -0.5),
        "gm_b_s": jnp.ones((GM_HEADS, CHUNK), f32) + 0.1 * jax.random.normal(ks[12], (GM_HEADS, CHUNK), f32),
        "w_out": nrm(ks[13], (MIX_WIDTH, D_MODEL), MIX_WIDTH ** -0.5),
        "norm_xa_g": gain(ks[14], D_MODEL),
        "mem_norm_g": gain(ks[15], D_MODEL),
        "xa_wq": nrm(ks[16], (D_MODEL, D_MODEL), D_MODEL ** -0.5),
        "xa_wkv": nrm(ks[17], (D_MODEL, 2 * D_MODEL), D_MODEL ** -0.5),
        "xa_wo": nrm(ks[18], (D_MODEL, D_MODEL), D_MODEL ** -0.5),
        "norm_ffn_g": gain(ks[19], D_MODEL),
        "ffn_w_gate_up": nrm(ks[20], (D_MODEL, 2 * FFN_HIDDEN), D_MODEL ** -0.5),
        "ffn_w_down": nrm(ks[21], (FFN_HIDDEN, D_MODEL), FFN_HIDDEN ** -0.5),
        "final_norm_g": gain(ks[22], D_MODEL),
    }


def reference(x, mem, norm_mix_g, w_in, b_in, conv_w, conv_b, conv_ln_g, conv_ln_b,
              gm_ln_g, gm_ln_b, gm_w_s, gm_b_s, w_out, norm_xa_g, mem_norm_g,
              xa_wq, xa_wkv, xa_wo, norm_ffn_g, ffn_w_gate_up, ffn_w_down,
              final_norm_g):
    h = x
    mn = rmsnorm(mem, mem_norm_g)
    for _ in range(DEPTH):
        hn = rmsnorm(h, norm_mix_g)
        z = hn @ w_in + b_in
        za, zg, zu, zv = jnp.split(
            z, [CONV_WIDTH, 2 * CONV_WIDTH, 2 * CONV_WIDTH + GM_WIDTH], axis=-1)
        conv_out = conformer_conv_group(za, zg, conv_w, conv_b, conv_ln_g, conv_ln_b)
        gm_out = gmlp_group(zu, zv, gm_ln_g, gm_ln_b, gm_w_s, gm_b_s)
        h = h + jnp.concatenate([conv_out, gm_out], axis=-1) @ w_out
        h = h + cross_attention(rmsnorm(h, norm_xa_g), mn, xa_wq, xa_wkv, xa_wo)
        h = h + swiglu(rmsnorm(h, norm_ffn_g), ffn_w_gate_up, ffn_w_down)
    return rmsnorm(h, final_norm_g)
```

```python
import numpy as np
from contextlib import ExitStack
import concourse.bass as bass
import concourse.mybir as mybir
from concourse.bass_utils import run_bass_kernel_spmd

F32 = mybir.dt.float32
BF16 = mybir.dt.bfloat16
AF = mybir.ActivationFunctionType
ALU = mybir.AluOpType

NCORES = 8
D = 1024
TOK = 2048
NSUB = TOK // 128
MEM = 256
HID = 2816
NHC = HID // 128
FFN_GROUPS = [(0, 6), (6, 6), (12, 5), (17, 5)]
GMAX = 6
TS1 = 256
NT1 = TOK // TS1
KCONV = 31
RMS_EPS = 1e-6
LN_EPS = 1e-5


class Res:
    __slots__ = ("name", "w", "r")

    def __init__(self, name):
        self.name = name
        self.w = None
        self.r = {}


class Sched:
    NDS = 16

    def __init__(self, nc, es):
        self.nc = nc
        self.eng = dict(pe=nc.tensor, act=nc.scalar, dve=nc.vector, pool=nc.gpsimd, sp=nc.sync)
        self.sems = {}
        for k in self.eng:
            self.sems[k] = es.enter_context(nc.semaphore("s_" + k))
        self.cnt = {k: 0 for k in self.eng}
        self.seen = {k: {} for k in self.eng}
        self.logging = False
        self.log = {k: [] for k in self.eng}
        self.dq = {}
        for q in ("sp", "pool"):
            lst = []
            for i in range(self.NDS):
                key = "d_%s%d" % (q, i)
                self.sems[key] = es.enter_context(nc.semaphore(key))
                self.cnt[key] = 0
                lst.append(key)
            self.dq[q] = [lst, 0]

    def _where(self):
        import sys
        f = sys._getframe(2)
        names = []
        while f is not None and len(names) < 6:
            n = f.f_code.co_name
            if n == "build_nc":
                break
            if n not in ("<lambda>", "op", "mm_group", "run", "interleave", "step", "chain"):
                loc = f.f_locals
                tag = n
                for v in ("t", "i", "c", "s", "hd", "dc"):
                    if v in loc and isinstance(loc[v], int):
                        tag += " %s=%d" % (v, loc[v])
                names.append(tag)
            f = f.f_back
        return " < ".join(names)

    def _wait(self, e, deps):
        for key, val in sorted(deps):
            if key == e and e == "pe":
                continue
            if self.seen[e].get(key, 0) >= val:
                continue
            self.eng[e].wait_ge(self.sems[key], val)
            self.seen[e][key] = val

    def _deps(self, reads, writes):
        deps = set()
        for r in reads:
            if r.w is not None:
                deps.add(r.w)
        for w in writes:
            if w.w is not None:
                deps.add(w.w)
            for k, v in w.r.items():
                deps.add((k, v))
        return deps

    def _mark(self, mark, reads, writes):
        k, v = mark
        for r in reads:
            if r.r.get(k, 0) < v:
                r.r[k] = v
        for w in writes:
            w.w = mark
            w.r = {}

    def op(self, e, fn, reads=(), writes=(), signal=True):
        self._wait(e, self._deps(reads, writes))
        inst = fn(self.eng[e])
        if self.logging:
            self.log[e].append(self._where())
        if signal:
            self.cnt[e] += 1
            inst.then_inc(self.sems[e], 1)
            mark = (e, self.cnt[e])
        else:
            assert e == "pe"
            mark = (e, self.cnt[e] + 1)
        self._mark(mark, reads, writes)
        return inst

    def dma(self, q, out, in_, reads=(), writes=(), **kw):
        lst, nxt = self.dq[q]
        key = lst[nxt % self.NDS]
        self.dq[q][1] = nxt + 1
        deps = self._deps(reads, writes)
        if self.cnt[key] > 0:
            deps.add((key, self.cnt[key]))
        self._wait(q, deps)
        inst = self.eng[q].dma_start(out=out, in_=in_, **kw)
        self.cnt[key] += 16
        inst.then_inc(self.sems[key], 16)
        self._mark((key, self.cnt[key]), reads, writes)
        return inst

    def wait_all(self, e):
        deps = set()
        for k, v in self.cnt.items():
            if v > 0:
                deps.add((k, v))
        self._wait(e, deps)

    def barrier(self, dma=True):
        for e in self.eng:
            if dma:
                self.wait_all(e)
            else:
                self._wait(e, {(k, self.cnt[k]) for k in self.eng if self.cnt[k] > 0})


class PsPool:
    def __init__(self, nc, es, n=8):
        self.n = n
        self.t = [es.enter_context(nc.psum_tensor("psb%d" % i, [128, 512], F32)) for i in range(n)]
        self.res = [Res("psb%d" % i) for i in range(n)]
        self.held = [False] * n
        self.freeq = list(range(n))

    def get(self):
        if not self.freeq:
            raise RuntimeError("PSUM banks exhausted")
        i = self.freeq.pop(0)
        self.held[i] = True
        return i

    def free(self, i):
        assert self.held[i]
        self.held[i] = False
        self.freeq.append(i)


LOGGING = False


def run(gen):
    for _ in gen:
        pass


def interleave(*items):
    active = [[g, n] for g, n in items if g is not None]
    while active:
        for it in list(active):
            for _ in range(it[1]):
                try:
                    next(it[0])
                except StopIteration:
                    active.remove(it)
                    break


def step(gen, n):
    if gen is None:
        return
    for _ in range(n):
        try:
            next(gen)
        except StopIteration:
            return


def chain(*gens):
    for g in gens:
        for _ in g:
            yield


def build_nc():
    nc = bass.Bass("TRN2", target_bir_lowering=False)

    def din(name, shape):
        return nc.dram_tensor(name, list(shape), F32, kind="ExternalInput").ap()

    x_d = din("x", [TOK, D])
    xh_d = din("xh", [128, D])
    mem_d = din("mem", [MEM, D])
    w_in_d = din("w_in", [D, 2048])
    w_out_d = din("w_out", [D, D])
    wq_d = din("xa_wq", [D, D])
    wkv_d = din("xa_wkv", [D, 2 * D])
    wo_d = din("xa_wo", [D, D])
    wgu_d = din("ffn_w_gate_up", [D, 2 * HID])
    wd_d = din("ffn_w_down", [HID, D])
    g_mix_d = din("norm_mix_g", [D])
    g_xa_d = din("norm_xa_g", [D])
    g_ffn_d = din("norm_ffn_g", [D])
    g_fin_d = din("final_norm_g", [D])
    g_mem_d = din("mem_norm_g", [D])
    vec_d = din("vec", [128, 40])
    cwp_d = din("cwp", [128, 16, 8])
    e32_d = din("E32", [128, 32])
    wst_d = din("wst", [128, 8, 128])
    mask_d = din("maskT", [128, 8, 128])
    bs_d = din("bs_rep", [128, 4, 128])
    ident_d = din("ident", [128, 128])
    bv_d = din("bv_nat", [512])
    gg_d = din("gm_g_nat", [512])
    gbn_d = din("gm_b_nat", [512])
    out_d = nc.dram_tensor("out", [TOK, D], F32, kind="ExternalOutput").ap()

    es = ExitStack()
    with es:
        S = Sched(nc, es)
        S.logging = LOGGING
        nc._sched = S
        PS = PsPool(nc, es)

        def T(scope, name, shape, dt):
            return scope.enter_context(nc.sbuf_tensor("sb_" + name, list(shape), dt))

        def psf(b, n=512):
            return PS.t[b][:, 0:n]

        def psh(b, half, n):
            return PS.t[b][:, half * 256:half * 256 + n]

        def psb16(b):
            return PS.t[b][:].bitcast(BF16)

        def mm_group(out_ap, out_res, pairs, reads):
            n = len(pairs)
            for i, (l, r) in enumerate(pairs):
                S.op("pe", lambda e: e.matmul(out_ap, lhsT=l, rhs=r, start=(i == 0), stop=(i == n - 1)),
                     reads=reads, writes=[out_res], signal=(i == n - 1))

        h = T(es, "h", [128, NSUB, D], F32)
        r_h = [Res("h%d" % g) for g in range(NSUB)]
        identf = T(es, "identf", [128, 128], F32)
        identb = T(es, "identb", [128, 128], BF16)
        ones_b = T(es, "ones_b", [128, 128], BF16)
        vec = T(es, "vec", [128, 40], F32)
        stat = T(es, "stat", [128, 4, NSUB], F32)
        rstd = T(es, "rstd", [128, 4, NSUB], F32)
        stath = T(es, "stath", [128, 4], F32)
        r_ident, r_vec, r_ones = Res("ident"), Res("vec"), Res("ones")
        r_stat = [[Res("stat%d_%d" % (i, g)) for g in range(NSUB)] for i in range(4)]
        r_rstd = [[Res("rstd%d_%d" % (i, g)) for g in range(NSUB)] for i in range(4)]
        r_stath = Res("stath")

        S.dma("sp", identf[:], ident_d, writes=[r_ident])
        S.dma("sp", vec[:], vec_d, writes=[r_vec])
        S.op("dve", lambda e: e.tensor_copy(out=identb[:], in_=identf[:]), reads=[r_ident], writes=[r_ident])
        S.op("pool", lambda e: e.memset(ones_b[:], 1.0), writes=[r_ones])

        def bias(i):
            return vec[:, i:i + 1]

        def nt_scale(src_ap, src_res, rstd_ap, rstd_res, gb, r_gb, hs_bufs, hs_res, idx):
            hs = hs_bufs[idx % len(hs_bufs)]
            rh = hs_res[idx % len(hs_bufs)]
            S.op("dve", lambda e: e.scalar_tensor_tensor(out=hs[:], in0=src_ap, scalar=rstd_ap, in1=gb[:],
                                                          op0=ALU.mult, op1=ALU.mult),
                 reads=[src_res, rstd_res, r_gb], writes=[rh])

        def norm_transpose(src_ap, src_res, rstd_ap, rstd_res, gb, r_gb, hs_bufs, hs_res, idx, dst_ap, dst_res,
                           scale=True, evac="act"):
            hs = hs_bufs[idx % len(hs_bufs)]
            rh = hs_res[idx % len(hs_bufs)]
            if scale:
                nt_scale(src_ap, src_res, rstd_ap, rstd_res, gb, r_gb, hs_bufs, hs_res, idx)
            b = PS.get()
            pv = psb16(b)
            for c in range(8):
                S.op("pe", lambda e: e.transpose(out=pv[:, c * 128:(c + 1) * 128], in_=hs[:, c * 128:(c + 1) * 128],
                                                 identity=identb[:]),
                     reads=[rh, r_ident], writes=[PS.res[b]], signal=(c == 7))
            if evac == "act":
                S.op("act", lambda e: e.activation(out=dst_ap, in_=pv.rearrange("p (c t) -> p c t", c=8),
                                                   func=AF.Copy), reads=[PS.res[b]], writes=[dst_res])
            else:
                S.op("dve", lambda e: e.tensor_copy(out=dst_ap.bitcast(F32),
                                                    in_=psf(b).rearrange("p (c t) -> p c t", c=8)),
                     reads=[PS.res[b]], writes=[dst_res])
            PS.free(b)

        def finish_stats(si, lo, hi):
            rs = [r_stat[si][g] for g in range(lo, hi)]
            ws = [r_rstd[si][g] for g in range(lo, hi)]
            S.op("act", lambda e: e.activation(out=rstd[:, si, lo:hi], in_=stat[:, si, lo:hi], func=AF.Ln,
                                               scale=1.0 / D, bias=RMS_EPS), reads=rs, writes=ws)
            S.op("act", lambda e: e.activation(out=rstd[:, si, lo:hi], in_=rstd[:, si, lo:hi], func=AF.Exp,
                                               scale=-0.5), reads=ws, writes=ws)

        def sq_stats(junk, r_junk, si, g, eng="act"):
            if eng == "act":
                S.op("act", lambda e: e.activation(out=junk[:], in_=h[:, g, :], func=AF.Square,
                                                   accum_out=stat[:, si, g:g + 1]),
                     reads=[r_h[g]], writes=[r_junk, r_stat[si][g]])
            else:
                S.op("dve", lambda e: e.scalar_tensor_tensor(out=junk[:], in0=h[:, g, :], scalar=1.0, in1=h[:, g, :],
                                                              op0=ALU.mult, op1=ALU.mult,
                                                              accum_out=stat[:, si, g:g + 1]),
                     reads=[r_h[g]], writes=[r_junk, r_stat[si][g]])

        s1 = ExitStack()
        with s1:
            w_in = T(s1, "w_in", [128, 8, 2048], BF16)
            w_out = T(s1, "w_out", [128, 8, D], BF16)
            r_wout = Res("w_out")
            r_winb = [Res("w_in%d" % i) for i in range(4)]
            w_in_v = w_in_d.rearrange("(k p) n -> p k n", p=128)
            for blk in (1, 0, 2, 3):
                S.dma("pool", w_in[:, :, blk * 512:(blk + 1) * 512], w_in_v[:, :, blk * 512:(blk + 1) * 512],
                      writes=[r_winb[blk]])
            win_mark = r_winb[3].w
            gb = T(s1, "gb1", [128, D], F32)
            r_gb = Res("gb1")
            cwp = T(s1, "cwp", [128, 16, 8], F32)
            e32 = T(s1, "e32", [128, 32], F32)
            r_cw = Res("cwp")
            r_e32 = Res("e32")
            wsT = T(s1, "wsT", [128, 8, 128], BF16)
            r_wsT = Res("wsT")
            bs = T(s1, "bs", [128, 4, 128], F32)
            r_bs = Res("bs")
            S.dma("sp", gb[:], g_mix_d.partition_broadcast(128), writes=[r_gb])
            junk = T(s1, "junk1", [128, D], BF16)
            r_junk = Res("junk1")
            hs_bufs = [T(s1, "hs1_%d" % i, [128, D], BF16) for i in range(2)]
            hs_res = [Res("hs1_%d" % i) for i in range(2)]
            hnTh = T(s1, "hnTh", [128, 8, 128], BF16)
            r_hnTh = Res("hnTh")
            for g in range(2):
                S.dma("sp", h[:, g, :], x_d[g * 128:(g + 1) * 128, :], writes=[r_h[g]])
            s1s = ExitStack()
            with s1s:
                xhalo = T(s1s, "xhalo", [128, D], F32)
                r_xhalo = Res("xhalo")
                S.dma("sp", xhalo[:], xh_d, writes=[r_xhalo])
                S.dma("sp", cwp[:], cwp_d, writes=[r_cw])
                S.dma("sp", e32[:], e32_d, writes=[r_e32])
                S.dma("sp", bs[:], bs_d, writes=[r_bs])
                wst_f = T(s1s, "wst_f", [128, 8, 128], F32)
                mask_f = T(s1s, "mask_f", [128, 8, 128], F32)
                r_wstf, r_maskf = Res("wstf"), Res("maskf")
                S.dma("sp", wst_f[:], wst_d, writes=[r_wstf])
                S.dma("sp", mask_f[:], mask_d, writes=[r_maskf])
                S.op("dve", lambda e: e.tensor_tensor(out=wsT[:], in0=wst_f[:], in1=mask_f[:], op=ALU.mult),
                     reads=[r_wstf, r_maskf], writes=[r_wsT])
                S.op("act", lambda e: e.activation(out=junk[:], in_=xhalo[:], func=AF.Square,
                                                   accum_out=stath[:, 0:1]),
                     reads=[r_xhalo], writes=[r_junk, r_stath])
                S.op("act", lambda e: e.activation(out=stath[:, 1:2], in_=stath[:, 0:1], func=AF.Ln, scale=1.0 / D,
                                                   bias=RMS_EPS), reads=[r_stath], writes=[r_stath])
                S.op("act", lambda e: e.activation(out=stath[:, 1:2], in_=stath[:, 1:2], func=AF.Exp, scale=-0.5),
                     reads=[r_stath], writes=[r_stath])
                norm_transpose(xhalo[:], r_xhalo, stath[:, 1:2], r_stath, gb, r_gb, hs_bufs, hs_res, 0, hnTh[:],
                               r_hnTh)
                for e_ in S.eng:
                    S._wait(e_, {("dve", S.cnt["dve"]), ("act", S.cnt["act"])})
            S._wait("sp", {r_winb[0].w})
            for g in range(2, 4):
                S.dma("sp", h[:, g, :], x_d[g * 128:(g + 1) * 128, :], writes=[r_h[g]])
            S.dma("pool", w_out[:], w_out_d.rearrange("(k p) n -> p k n", p=128), writes=[r_wout])

            WP = T(s1, "WP", [128, 16, 8, 32], BF16)
            r_WP = Res("WP")
            S.op("pool", lambda e: e.tensor_tensor(
                out=WP[:].rearrange("p q j c -> p (q j) c"),
                in0=e32[:].unsqueeze(1).broadcast_to([128, 128, 32]),
                in1=cwp[:].rearrange("p q j -> p (q j)").unsqueeze(2).broadcast_to([128, 128, 32]), op=ALU.mult),
                 reads=[r_cw, r_e32], writes=[r_WP])
            WX = 128 + TS1
            WC = WX - 2
            AXT = T(s1, "AXT", [128, 16, WX], BF16)
            r_AXT = [[Res("AXT%d_%d" % (qq, jj)) for jj in range(4)] for qq in range(4)]

            def relayout(axc, raxc):
                for qq in range(4):
                    for jj in range(4):
                        dst = AXT[jj * 32:(jj + 1) * 32, :, 0:WC].rearrange("p (c q) w -> p c q w", q=4)[:, :, qq, :]
                        S.dma("sp", dst, axc[qq * 32:(qq + 1) * 32, :, jj:jj + WC], reads=list(raxc),
                              writes=[r_AXT[qq][jj]])

            hnT = [T(s1, "hnT1_0", [128, 8, TS1], BF16)] * 2
            r_hnT = [Res("hnT1_0")] * 2
            ax = [T(s1, "ax%d" % i, [128, 4, WX + 2], BF16) for i in range(2)]
            r_ax = [[Res("ax%d_%d" % (i, c)) for c in range(4)] for i in range(2)]
            for i in range(2):
                S.op("pool", lambda e: e.memset(ax[i][:], 0.0), writes=r_ax[i])
            sig = [T(s1, "sig%d" % i, [128, TS1], BF16) for i in range(2)]
            r_sig = [Res("sig%d" % i) for i in range(2)]
            u_t = [T(s1, "u_t%d" % i, [128, 4, TS1], BF16) for i in range(2)]
            r_u = [Res("u%d" % i) for i in range(2)]
            bvt = T(s1, "bvt", [128, 512], F32)
            ggt = T(s1, "ggt", [128, 512], F32)
            gbt = T(s1, "gbt", [128, 512], F32)
            r_bvt, r_ggt, r_gbt = Res("bvt"), Res("ggt"), Res("gbt")
            S.dma("sp", bvt[:], bv_d.partition_broadcast(128), writes=[r_bvt])
            S.dma("sp", ggt[:], gg_d.partition_broadcast(128), writes=[r_ggt])
            S.dma("sp", gbt[:], gbn_d.partition_broadcast(128), writes=[r_gbt])
            zb = [T(s1, "zb%d" % i, [128, 512], F32) for i in range(2)]
            r_zb = [Res("zb%d" % i) for i in range(2)]
            vst = T(s1, "vst", [128, 8], F32)
            r_vst = Res("vst")
            By1 = T(s1, "By1", [128, 4, TS1], BF16)
            By2 = T(s1, "By2", [128, 4, TS1], BF16)
            r_By1 = [Res("By1_c%d" % c) for c in range(4)]
            r_By2 = [Res("By2_c%d" % c) for c in range(4)]
            vtm = [T(s1, "vtm%d" % i, [128, 512], BF16) for i in range(2)]
            r_vtm = [Res("vtm%d" % i) for i in range(2)]
            mixT = T(s1, "mixT", [128, 8, TS1], BF16)
            r_mixc, r_mixg = Res("mixT_conv"), Res("mixT_gm")
            lnb = {}
            for nm in ("E",):
                lnb[nm] = (T(s1, "ln_mean" + nm, [128, TS1], F32), T(s1, "ln_rstd" + nm, [128, TS1], F32),
                           Res("ln_mean" + nm), Res("ln_rstd" + nm))
            t1 = [T(s1, "t1_%d" % i, [128, TS1], F32) for i in range(2)]
            r_t1 = [Res("t1_%d" % i) for i in range(2)]
            gmt_all = T(s1, "gmt", [128, 2 * TS1], F32)
            gmt = gmt_all[:, 0:TS1]
            gmt_junk = gmt_all[:].bitcast(BF16)
            r_gmt = Res("gmt")
            nsub = TS1 // 128

            def stats1(t):
                for s in range(nsub):
                    sq_stats(junk, r_junk, 0, t * nsub + s)
                finish_stats(0, t * nsub, (t + 1) * nsub)

            stats1(0)
            def glu_chunk(c, hn_ap, r_hn, n, dst_list, maskit=False):
                sg = sig[c % 2]
                rs = r_sig[c % 2]
                b = PS.get()
                mm_group(psf(b, n), PS.res[b],
                         [(w_in[:, k, (4 + c) * 128:(5 + c) * 128], hn_ap(k)) for k in range(8)], [r_winb[1], r_hn])
                S.op("act", lambda e: e.activation(out=sg[:, 0:n], in_=psf(b, n), func=AF.Sigmoid, bias=bias(4 + c)),
                     reads=[PS.res[b], r_vec], writes=[rs])
                PS.free(b)
                b = PS.get()
                mm_group(psf(b, n), PS.res[b],
                         [(w_in[:, k, c * 128:(c + 1) * 128], hn_ap(k)) for k in range(8)], [r_winb[0], r_hn])
                for (dst, rd, lo, hi) in dst_list:
                    S.op("dve", lambda e: e.scalar_tensor_tensor(out=dst, in0=PS.t[b][:, lo:hi], scalar=bias(c),
                                                                  in1=sg[:, lo:hi], op0=ALU.add, op1=ALU.mult),
                         reads=[PS.res[b], rs, r_vec], writes=[rd])
                    if maskit:
                        S.op("dve", lambda e: e.tensor_scalar(out=dst, in0=dst, scalar1=vec[:, 36:37], scalar2=None,
                                                              op0=ALU.mult), reads=[rd, r_vec], writes=[rd])
                PS.free(b)

            def A1s(t):
                for s in range(nsub):
                    g = t * nsub + s
                    nt_scale(h[:, g, :], r_h[g], rstd[:, 0, g:g + 1], r_rstd[0][g], gb, r_gb, hs_bufs, hs_res, g)

            def A1(t, scale=True):
                hb, rhb = hnT[t % 2], r_hnT[t % 2]
                for s in range(nsub):
                    g = t * nsub + s
                    norm_transpose(h[:, g, :], r_h[g], rstd[:, 0, g:g + 1], r_rstd[0][g], gb, r_gb, hs_bufs, hs_res,
                                   g, hb[:, :, s * 128:(s + 1) * 128], rhb, scale=scale, evac="dve")
                    yield

            def B1(t):
                hb, rhb = hnT[t % 2], r_hnT[t % 2]
                axc, raxc = ax[t % 2], r_ax[t % 2]
                axn, raxn = ax[(t + 1) % 2], r_ax[(t + 1) % 2]
                hn_ap = lambda k: hb[:, k, :]
                for c in range(4):
                    dsts = [(axc[:, c, 128:128 + TS1], raxc[c], 0, TS1)]
                    if t + 1 < NT1:
                        dsts.append((axn[:, c, 0:128], raxn[c], TS1 - 128, TS1))
                    glu_chunk(c, hn_ap, rhb, TS1, dsts)
                    if c == 3:
                        relayout(axc, raxc)
                    yield
                for c in range(4):
                    b = PS.get()
                    mm_group(psf(b, TS1), PS.res[b],
                             [(w_in[:, k, (8 + c) * 128:(9 + c) * 128], hn_ap(k)) for k in range(8)],
                             [r_winb[2], rhb])
                    S.op("act", lambda e: e.activation(out=u_t[t % 2][:, c, :], in_=psf(b, TS1),
                                                       func=AF.Gelu_apprx_tanh, bias=bias(8 + c)),
                         reads=[PS.res[b], r_vec], writes=[r_u[t % 2]])
                    PS.free(b)
                    yield
                for s in range(nsub):
                    b = PS.get()
                    mm_group(psf(b), PS.res[b],
                             [(hb[:, k, s * 128:(s + 1) * 128], w_in[:, k, 1536:2048]) for k in range(8)],
                             [r_winb[3], rhb])
                    S.op("dve", lambda e: e.tensor_tensor(out=zb[s][:], in0=psf(b), in1=bvt[:], op=ALU.add),
                         reads=[PS.res[b], r_bvt], writes=[r_zb[s]])
                    PS.free(b)
                    S.op("act", lambda e: e.activation(out=zb[s][:], in_=zb[s][:], func=AF.Gelu_apprx_tanh,
                                                       accum_out=vst[:, s:s + 1]),
                         reads=[r_zb[s]], writes=[r_zb[s], r_vst])
                    S.op("act", lambda e: e.activation(out=junk[:, 0:512], in_=zb[s][:], func=AF.Square,
                                                       accum_out=vst[:, 2 + s:3 + s]),
                         reads=[r_zb[s]], writes=[r_junk, r_vst])
                    yield
                S.op("dve", lambda e: e.tensor_scalar(out=vst[:, 4:6], in0=vst[:, 0:2], scalar1=1.0 / 512, scalar2=None,
                                                      op0=ALU.mult), reads=[r_vst], writes=[r_vst])
                S.op("dve", lambda e: e.tensor_tensor(out=vst[:, 6:8], in0=vst[:, 4:6], in1=vst[:, 4:6], op=ALU.mult),
                     reads=[r_vst], writes=[r_vst])
                S.op("dve", lambda e: e.scalar_tensor_tensor(out=vst[:, 6:8], in0=vst[:, 2:4], scalar=1.0 / 512,
                                                              in1=vst[:, 6:8], op0=ALU.mult, op1=ALU.subtract),
                     reads=[r_vst], writes=[r_vst])
                S.op("act", lambda e: e.activation(out=vst[:, 6:8], in_=vst[:, 6:8], func=AF.Ln, bias=LN_EPS),
                     reads=[r_vst], writes=[r_vst])
                S.op("act", lambda e: e.activation(out=vst[:, 6:8], in_=vst[:, 6:8], func=AF.Exp, scale=-0.5),
                     reads=[r_vst], writes=[r_vst])
                for s in range(nsub):
                    S.op("dve", lambda e: e.scalar_tensor_tensor(out=zb[s][:], in0=zb[s][:], scalar=vst[:, 4 + s:5 + s],
                                                                  in1=ggt[:], op0=ALU.subtract, op1=ALU.mult),
                         reads=[r_zb[s], r_vst, r_ggt], writes=[r_zb[s]])
                    S.op("dve", lambda e: e.scalar_tensor_tensor(out=vtm[s][:], in0=zb[s][:], scalar=vst[:, 6 + s:7 + s],
                                                                  in1=gbt[:], op0=ALU.mult, op1=ALU.add),
                         reads=[r_zb[s], r_vst, r_gbt], writes=[r_vtm[s]])
                yield

            def ln_stats(nm, src, r_src, srcsq, r_sq):
                mean, rs_, r_m, r_r = lnb[nm]
                b1 = PS.get()
                mm_group(psf(b1, TS1), PS.res[b1], [(ones_b[:], src[:, c, :]) for c in range(4)], [r_ones] + list(r_src))
                b2 = PS.get()
                mm_group(psf(b2, TS1), PS.res[b2], [(ones_b[:], srcsq[:, c, :]) for c in range(4)], [r_ones] + list(r_sq))
                S.op("act", lambda e: e.activation(out=mean[:], in_=psf(b1, TS1), func=AF.Copy, scale=1.0 / 512),
                     reads=[PS.res[b1]], writes=[r_m])
                S.op("act", lambda e: e.activation(out=rs_[:], in_=psf(b1, TS1), func=AF.Square, scale=1.0 / 512),
                     reads=[PS.res[b1]], writes=[r_r])
                PS.free(b1)
                S.op("dve", lambda e: e.scalar_tensor_tensor(out=rs_[:], in0=psf(b2, TS1), scalar=1.0 / 512,
                                                              in1=rs_[:], op0=ALU.mult, op1=ALU.subtract),
                     reads=[PS.res[b2], r_r], writes=[r_r])
                PS.free(b2)
                S.op("act", lambda e: e.activation(out=rs_[:], in_=rs_[:], func=AF.Ln, bias=LN_EPS),
                     reads=[r_r], writes=[r_r])
                S.op("act", lambda e: e.activation(out=rs_[:], in_=rs_[:], func=AF.Exp, scale=-0.5),
                     reads=[r_r], writes=[r_r])

            def ln_apply(nm, src, r_src, func, gi, bi, dst_fn, r_dst):
                mean, rs_, r_m, r_r = lnb[nm]
                for c in range(4):
                    tt = t1[c % 2]
                    rt = r_t1[c % 2]
                    S.op("dve", lambda e: e.tensor_tensor(out=tt[:], in0=src[:, c, :], in1=mean[:],
                                                          op=ALU.subtract), reads=[r_src[c], r_m], writes=[rt])
                    S.op("dve", lambda e: e.tensor_tensor(out=tt[:], in0=tt[:], in1=rs_[:], op=ALU.mult),
                         reads=[rt, r_r], writes=[rt])
                    S.op("act", lambda e: e.activation(out=dst_fn(c), in_=tt[:], func=func, scale=bias(gi + c),
                                                       bias=bias(bi + c)), reads=[rt, r_vec], writes=[r_dst])
                    yield

            def C1(t):
                axc, raxc = ax[t % 2], r_ax[t % 2]
                for c in range(4):
                    b = PS.get()
                    for J in range(8):
                        for qq in range(4):
                            q = 4 * c + qq
                            S.op("pe", lambda e: e.matmul(PS.t[b][32 * qq:32 * qq + 32, 0:TS1], lhsT=WP[:, q, J, :],
                                                          rhs=AXT[:, q, 98 + 4 * J:98 + 4 * J + TS1],
                                                          start=(J == 0), stop=(J == 7), tile_position=(0, 32 * qq)),
                                 reads=[r_WP] + r_AXT[qq], writes=[PS.res[b]], signal=(J == 7 and qq == 3))
                    S.op("act", lambda e: e.activation(out=By1[:, c, :], in_=psf(b, TS1), func=AF.Identity,
                                                       bias=bias(16 + c)), reads=[PS.res[b], r_vec], writes=[r_By1[c]])
                    PS.free(b)
                    S.op("pool", lambda e: e.tensor_tensor(out=By2[:, c, :], in0=By1[:, c, :], in1=By1[:, c, :],
                                                          op=ALU.mult), reads=[r_By1[c]], writes=[r_By2[c]])
                    yield

            def F1(t):
                u_c, r_uc = u_t[t % 2], r_u[t % 2]
                bm2 = [PS.get() for _ in range(2)]
                bm = [bm2[0], bm2[0], bm2[1], bm2[1]]
                for s in range(nsub):
                    vt, rvt = vtm[s % 2], r_vtm[s % 2]
                    for c in range(4):
                        for hh in range(2):
                            hd = 2 * c + hh
                            last = (s == nsub - 1) and (hh == 1)
                            S.op("pe", lambda e: e.matmul(PS.t[bm[c]][hh * 64:(hh + 1) * 64,
                                                                       (c % 2) * 256 + s * 128:(c % 2) * 256 + (s + 1) * 128],
                                                          lhsT=vt[:, hd * 64:(hd + 1) * 64], rhs=wsT[:, hd, :],
                                                          start=True, stop=True),
                                 reads=[rvt, r_wsT], writes=[PS.res[bm[c]]], signal=last)
                    yield
                for c in range(4):
                    S.op("dve", lambda e: e.tensor_tensor(
                        out=gmt[:].rearrange("p (s t) -> p s t", s=nsub),
                        in0=psh(bm[c], c % 2, TS1).rearrange("p (s t) -> p s t", s=nsub),
                        in1=bs[:, c, :].unsqueeze(1).broadcast_to([128, nsub, 128]), op=ALU.add),
                         reads=[PS.res[bm[c]], r_bs], writes=[r_gmt])
                    S.op("dve", lambda e: e.tensor_tensor(out=mixT[:, 4 + c, :], in0=gmt[:], in1=u_c[:, c, :],
                                                          op=ALU.mult), reads=[r_gmt, r_uc], writes=[r_mixg])
                    if c % 2 == 1:
                        PS.free(bm[c])
                    yield

            def G1(t):
                for s in range(nsub):
                    g = t * nsub + s
                    for hf in range(2):
                        b = PS.get()
                        mm_group(psf(b), PS.res[b],
                                 [(mixT[:, k, s * 128:(s + 1) * 128], w_out[:, k, hf * 512:(hf + 1) * 512])
                                  for k in range(8)], [r_mixc, r_mixg, r_wout])
                        S.op("dve", lambda e: e.tensor_tensor(out=h[:, g, hf * 512:(hf + 1) * 512], in0=psf(b),
                                                              in1=h[:, g, hf * 512:(hf + 1) * 512], op=ALU.add),
                             reads=[PS.res[b], r_h[g]], writes=[r_h[g]])
                        PS.free(b)
                        yield
                    sq_stats(gmt_junk, r_gmt, 1, g, eng="dve")

            A1s(0)
            run(A1(0, scale=False))
            stats1(1)
            for c in range(4):
                glu_chunk(c, lambda k: hnTh[:, k, :], r_hnTh, 128, [(ax[0][:, c, 0:128], r_ax[0][c], 0, 128)],
                          maskit=True)
            gB0 = B1(0)
            step(gB0, 8)
            S._wait("sp", {win_mark})
            for g in range(4, NSUB):
                S.dma("sp", h[:, g, :], x_d[g * 128:(g + 1) * 128, :], writes=[r_h[g]])
            A1s(1)
            run(gB0)
            run(A1(1, scale=False))
            stats1(2)
            for t in range(NT1):
                if t + 2 < NT1:
                    A1s(t + 2)
                run(C1(t))
                ln_stats("E", By1, r_By1, By2, r_By2)
                gB = B1(t + 1) if t + 1 < NT1 else None
                step(gB, 4)
                gF = F1(t)
                step(gF, 1)
                step(gB, 1)
                step(gF, 1)
                step(gB, 1)
                step(gF, 1)
                step(gB, 1)
                run(ln_apply("E", By1, r_By1, AF.Silu, 20, 24, lambda c: mixT[:, c, :], r_mixc))
                run(gF)
                if gB is not None:
                    run(gB)
                if t + 3 < NT1:
                    stats1(t + 3)
                if t + 2 < NT1:
                    run(A1(t + 2, scale=False))
                run(G1(t))
            S.barrier()

        s23 = ExitStack()
        with s23:
            wg = [None, None]
            wu = [None, None]
            wd = [None, None]
            wg[0] = T(s23, "wg0", [128, 8, GMAX * 128], BF16)
            wu[0] = T(s23, "wu0", [128, 8, GMAX * 128], BF16)
            wd[0] = T(s23, "wd0", [128, GMAX, D], BF16)
            gb3 = T(s23, "gb3", [128, D], F32)
            r_gb3 = Res("gb3")
            r_wg = [Res("wg%d" % i) for i in range(2)]
            r_wu = [Res("wu%d" % i) for i in range(2)]
            r_wd = [Res("wd%d" % i) for i in range(2)]

            def load_group(gi):
                c0, G = FFN_GROUPS[gi]
                bi = gi % 2
                S.dma("pool", wg[bi][:, :, 0:G * 128],
                      wgu_d[:, c0 * 128:(c0 + G) * 128].rearrange("(k p) n -> p k n", p=128), writes=[r_wg[bi]])
                S.dma("pool", wu[bi][:, :, 0:G * 128],
                      wgu_d[:, HID + c0 * 128:HID + (c0 + G) * 128].rearrange("(k p) n -> p k n", p=128),
                      writes=[r_wu[bi]])
                S.dma("pool", wd[bi][:, 0:G, :],
                      wd_d[c0 * 128:(c0 + G) * 128, :].rearrange("(g p) n -> p g n", p=128), writes=[r_wd[bi]])

            s2 = ExitStack()
            with s2:
                wq = T(s2, "wq", [128, 8, D], BF16)
                wo = T(s2, "wo", [128, 8, D], BF16)
                KT = T(s2, "KT", [128, 8, MEM], BF16)
                Vt = T(s2, "Vt", [128, 2, D], BF16)
                r_wq, r_wo, r_KT, r_V = Res("wq"), Res("wo"), Res("KT"), Res("V")
                gb = T(s2, "gb2", [128, D], F32)
                r_gb = Res("gb2")
                hs_bufs = [T(s2, "hs2_%d" % i, [128, D], BF16) for i in range(4)]
                hs_res = [Res("hs2_%d" % i) for i in range(4)]
                junk = T(s2, "junk2", [128, D], BF16)
                r_junk = Res("junk2")
                s2a = ExitStack()
                with s2a:
                    wkv = T(s2a, "wkv", [128, 8, 2 * D], BF16)
                    r_wkh = [Res("wk%d" % i) for i in range(2)]
                    r_wvh = [Res("wv%d" % i) for i in range(2)]
                    wkv_v = wkv_d.rearrange("(k p) n -> p k n", p=128)
                    for i in range(2):
                        S.dma("pool", wkv[:, :, i * 512:(i + 1) * 512], wkv_v[:, :, i * 512:(i + 1) * 512],
                              writes=[r_wkh[i]])
                    for i in range(2):
                        S.dma("pool", wkv[:, :, D + i * 512:D + (i + 1) * 512],
                              wkv_v[:, :, D + i * 512:D + (i + 1) * 512], writes=[r_wvh[i]])
                    S.dma("pool", wq[:], wq_d.rearrange("(k p) n -> p k n", p=128), writes=[r_wq])
                    S.dma("pool", wo[:], wo_d.rearrange("(k p) n -> p k n", p=128), writes=[r_wo])
                    load_group(0)
                    S.dma("sp", gb3[:], g_ffn_d.partition_broadcast(128), writes=[r_gb3])
                    msb = T(s2a, "msb", [128, 2, D], F32)
                    r_msb = Res("msb")
                    gbm = T(s2a, "gbm", [128, D], F32)
                    r_gbm = Res("gbm")
                    mnT = T(s2a, "mnT", [128, 8, MEM], BF16)
                    r_mnT = Res("mnT")
                    S.dma("sp", msb[:], mem_d.rearrange("(c p) n -> p c n", p=128), writes=[r_msb])
                    S.dma("sp", gbm[:], g_mem_d.partition_broadcast(128), writes=[r_gbm])
                    S.dma("sp", gb[:], g_xa_d.partition_broadcast(128), writes=[r_gb])
                    for mc in range(2):
                        S.op("act", lambda e: e.activation(out=junk[:], in_=msb[:, mc, :], func=AF.Square,
                                                           accum_out=stath[:, 2 + mc:3 + mc]),
                             reads=[r_msb], writes=[r_junk, r_stath])
                    S.op("act", lambda e: e.activation(out=stath[:, 2:4], in_=stath[:, 2:4], func=AF.Ln,
                                                       scale=1.0 / D, bias=RMS_EPS), reads=[r_stath], writes=[r_stath])
                    S.op("act", lambda e: e.activation(out=stath[:, 2:4], in_=stath[:, 2:4], func=AF.Exp,
                                                       scale=-0.5), reads=[r_stath], writes=[r_stath])
                    finish_stats(1, 0, NSUB)
                    for mc in range(2):
                        norm_transpose(msb[:, mc, :], r_msb, stath[:, 2 + mc:3 + mc], r_stath, gbm, r_gbm, hs_bufs,
                                       hs_res, mc, mnT[:, :, mc * 128:(mc + 1) * 128], r_mnT)
                    for dc in range(8):
                        b = PS.get()
                        mm_group(psf(b, MEM), PS.res[b],
                                 [(wkv[:, k, dc * 128:(dc + 1) * 128], mnT[:, k, :]) for k in range(8)],
                                 [r_wkh[dc // 4], r_mnT])
                        S.op("act", lambda e: e.activation(out=KT[:, dc, :], in_=psf(b, MEM), func=AF.Copy),
                             reads=[PS.res[b]], writes=[r_KT])
                        PS.free(b)
                    for mc in range(2):
                        for hf in range(2):
                            b = PS.get()
                            mm_group(psf(b), PS.res[b],
                                     [(mnT[:, k, mc * 128:(mc + 1) * 128],
                                       wkv[:, k, D + hf * 512:D + (hf + 1) * 512]) for k in range(8)], [r_wvh[hf], r_mnT])
                            S.op("act", lambda e: e.activation(out=Vt[:, mc, hf * 512:(hf + 1) * 512], in_=psf(b),
                                                               func=AF.Copy), reads=[PS.res[b]], writes=[r_V])
                            PS.free(b)
                    S.barrier(dma=False)
                hnT = [T(s2, "hnT2_0", [128, 8, 512], BF16)] * 2
                r_hnT = [Res("hnT2_0")] * 2
                qT = [T(s2, "qT%d" % i, [128, 8, 512], BF16) for i in range(2)]
                r_qT = [[Res("qT%d_%d" % (i, dc)) for dc in range(8)] for i in range(2)]
                PT = T(s2, "PT", [128, 4, 2, 512], BF16)
                rden = T(s2, "rden", [128, 4, 512], F32)
                oT = T(s2, "oT", [128, 8, 512], BF16)
                r_PT = [Res("PT%d" % i) for i in range(4)]
                r_rden = [Res("rden%d" % i) for i in range(4)]
                r_oT = Res("oT")
                SCALE = 256 ** -0.5
                NT2 = TOK // 512

                def A2s(t):
                    for s in range(4):
                        g = t * 4 + s
                        nt_scale(h[:, g, :], r_h[g], rstd[:, 1, g:g + 1], r_rstd[1][g], gb, r_gb, hs_bufs, hs_res, g)

                def A2(t, scale=True):
                    hb, rhb = hnT[t % 2], r_hnT[t % 2]
                    for s in range(4):
                        g = t * 4 + s
                        norm_transpose(h[:, g, :], r_h[g], rstd[:, 1, g:g + 1], r_rstd[1][g], gb, r_gb, hs_bufs,
                                       hs_res, g, hb[:, :, s * 128:(s + 1) * 128], rhb, scale=scale, evac="dve")
                        yield

                def Q2(t):
                    hb, rhb = hnT[t % 2], r_hnT[t % 2]
                    for dc in range(8):
                        b = PS.get()
                        mm_group(psf(b), PS.res[b],
                                 [(wq[:, k, dc * 128:(dc + 1) * 128], hb[:, k, :]) for k in range(8)], [r_wq, rhb])
                        if dc % 2 == 0:
                            S.op("act", lambda e: e.activation(out=qT[t % 2][:, dc, :], in_=psf(b), func=AF.Copy),
                                 reads=[PS.res[b]], writes=[r_qT[t % 2][dc]])
                        else:
                            S.op("dve", lambda e: e.tensor_copy(out=qT[t % 2][:, dc, :], in_=psf(b)),
                                 reads=[PS.res[b]], writes=[r_qT[t % 2][dc]])
                        PS.free(b)
                        yield

                def SH2(t):
                    q, rq = qT[t % 2], r_qT[t % 2]

                    def sc(hd):
                        for mc in range(2):
                            b = PS.get()
                            mm_group(psf(b), PS.res[b],
                                     [(KT[:, 2 * hd + i, mc * 128:(mc + 1) * 128], q[:, 2 * hd + i, :])
                                      for i in range(2)], [r_KT, rq[2 * hd], rq[2 * hd + 1]])
                            S.op("act", lambda e: e.activation(out=PT[:, hd, mc, :], in_=psf(b), func=AF.Exp,
                                                               scale=SCALE), reads=[PS.res[b]], writes=[r_PT[hd]])
                            PS.free(b)

                    def dn(hd):
                        b = PS.get()
                        mm_group(psf(b), PS.res[b], [(ones_b[:], PT[:, hd, mc, :]) for mc in range(2)],
                                 [r_ones, r_PT[hd]])
                        S.op("act", lambda e: e.activation(out=rden[:, hd, :], in_=psf(b), func=AF.Ln),
                             reads=[PS.res[b]], writes=[r_rden[hd]])
                        PS.free(b)
                        S.op("act", lambda e: e.activation(out=rden[:, hd, :], in_=rden[:, hd, :], func=AF.Exp,
                                                           scale=-1.0), reads=[r_rden[hd]], writes=[r_rden[hd]])

                    def pv_(hd):
                        for i in range(2):
                            dc = 2 * hd + i
                            b = PS.get()
                            mm_group(psf(b), PS.res[b],
                                     [(Vt[:, mc, dc * 128:(dc + 1) * 128], PT[:, hd, mc, :]) for mc in range(2)],
                                     [r_V, r_PT[hd]])
                            S.op("dve", lambda e: e.tensor_tensor(out=oT[:, dc, :], in0=psf(b), in1=rden[:, hd, :],
                                                                  op=ALU.mult),
                                 reads=[PS.res[b], r_rden[hd]], writes=[r_oT])
                            PS.free(b)

                    seq = [(sc, 0), (sc, 1), (sc, 2), (dn, 0), (sc, 3), (dn, 1), (pv_, 0), (dn, 2), (pv_, 1),
                           (dn, 3), (pv_, 2), (pv_, 3)]
                    for fn, hd in seq:
                        fn(hd)
                        yield

                def O2(t):
                    for s in range(4):
                        g = t * 4 + s
                        for hf in range(2):
                            b = PS.get()
                            mm_group(psf(b), PS.res[b],
                                     [(oT[:, k, s * 128:(s + 1) * 128], wo[:, k, hf * 512:(hf + 1) * 512])
                                      for k in range(8)], [r_oT, r_wo])
                            S.op("dve", lambda e: e.tensor_tensor(out=h[:, g, hf * 512:(hf + 1) * 512], in0=psf(b),
                                                                  in1=h[:, g, hf * 512:(hf + 1) * 512], op=ALU.add),
                                 reads=[PS.res[b], r_h[g]], writes=[r_h[g]])
                            PS.free(b)
                            yield
                        sq_stats(junk, r_junk, 2, g)

                A2s(0)
                run(A2(0, scale=False))
                run(Q2(0))
                A2s(1)
                run(A2(1, scale=False))
                for t in range(NT2):
                    nxt = Q2(t + 1) if t + 1 < NT2 else None
                    if t + 2 < NT2:
                        A2s(t + 2)
                    interleave((SH2(t), 3), (nxt, 2))
                    if t + 2 < NT2:
                        run(A2(t + 2, scale=False))
                    run(O2(t))
                finish_stats(2, 0, NSUB)
                S.barrier()

            s3 = ExitStack()
            with s3:
                gb, r_gb = gb3, r_gb3
                gbf = T(s3, "gbf", [128, D], F32)
                r_gbf = Res("gbf")
                S.dma("sp", gbf[:], g_fin_d.partition_broadcast(128), writes=[r_gbf])
                junk = T(s3, "junk3", [128, D], BF16)
                r_junk = Res("junk3")
                hsall = T(s3, "hs3", [128, 4, D], BF16)
                hs_bufs = [hsall[:, i, :] for i in range(4)]
                hs_res = [Res("hs3_%d" % i) for i in range(4)]
                yb = [hsall[:, 2 * j:2 * j + 2, :].rearrange("p a n -> p (a n)").bitcast(F32) for j in range(2)]
                r_yb = [Res("yb%d" % j) for j in range(2)]
                s3a = ExitStack()
                with s3a:
                    wg[1] = T(s3a, "wg1", [128, 8, GMAX * 128], BF16)
                    wu[1] = T(s3a, "wu1", [128, 8, GMAX * 128], BF16)
                    wd[1] = T(s3a, "wd1", [128, GMAX, D], BF16)
                    load_group(1)
                    hnT3 = T(s3a, "hnT3", [128, 8, TOK], BF16)
                    r_hnT3 = [Res("hnT3_%d" % g) for g in range(NSUB)]
                    sgb = [T(s3a, "sg%d" % i, [128, 512], F32) for i in range(2)]
                    r_sg = [Res("sg%d" % i) for i in range(2)]
                    hmid = [T(s3a, "hmid%d" % i, [128, GMAX, 512], BF16) for i in range(2)]
                    r_hmid = [Res("hmid%d" % i) for i in range(2)]
                    NT3 = TOK // 512

                    def A3s(t):
                        for s in range(4):
                            g = t * 4 + s
                            nt_scale(h[:, g, :], r_h[g], rstd[:, 2, g:g + 1], r_rstd[2][g], gb, r_gb, hs_bufs,
                                     hs_res, g)

                    def A3(t, scale=True):
                        for s in range(4):
                            g = t * 4 + s
                            norm_transpose(h[:, g, :], r_h[g], rstd[:, 2, g:g + 1], r_rstd[2][g], gb, r_gb, hs_bufs,
                                           hs_res, g, hnT3[:, :, g * 128:(g + 1) * 128], r_hnT3[g], scale=scale)
                            yield

                    steps = [(gi, t) for gi in range(len(FFN_GROUPS)) for t in range(NT3)]

                    def GU3(i):
                        gi, t = steps[i]
                        c0, G = FFN_GROUPS[gi]
                        bi = gi % 2
                        hm, rhm = hmid[i % 2], r_hmid[i % 2]
                        rds = [r_hnT3[t * 4 + s] for s in range(4)]
                        for c in range(G):
                            bg = PS.get()
                            mm_group(psf(bg), PS.res[bg],
                                     [(wg[bi][:, k, c * 128:(c + 1) * 128], hnT3[:, k, t * 512:(t + 1) * 512])
                                      for k in range(8)], [r_wg[bi]] + rds)
                            bu = PS.get()
                            mm_group(psf(bu), PS.res[bu],
                                     [(wu[bi][:, k, c * 128:(c + 1) * 128], hnT3[:, k, t * 512:(t + 1) * 512])
                                      for k in range(8)], [r_wu[bi]] + rds)
                            sg, rsg = sgb[c % 2], r_sg[c % 2]
                            S.op("act", lambda e: e.activation(out=sg[:], in_=psf(bg), func=AF.Silu),
                                 reads=[PS.res[bg]], writes=[rsg])
                            PS.free(bg)
                            S.op("dve", lambda e: e.tensor_tensor(out=hm[:, c, :], in0=psf(bu), in1=sg[:],
                                                                  op=ALU.mult), reads=[PS.res[bu], rsg], writes=[rhm])
                            PS.free(bu)
                            yield

                    def DN3(i):
                        gi, t = steps[i]
                        c0, G = FFN_GROUPS[gi]
                        bi = gi % 2
                        hm, rhm = hmid[i % 2], r_hmid[i % 2]
                        for s in range(4):
                            g = t * 4 + s
                            for hf in range(2):
                                b = PS.get()
                                mm_group(psf(b), PS.res[b],
                                         [(hm[:, c, s * 128:(s + 1) * 128], wd[bi][:, c, hf * 512:(hf + 1) * 512])
                                          for c in range(G)], [rhm, r_wd[bi]])
                                S.op("dve", lambda e: e.tensor_tensor(out=h[:, g, hf * 512:(hf + 1) * 512],
                                                                      in0=psf(b),
                                                                      in1=h[:, g, hf * 512:(hf + 1) * 512],
                                                                      op=ALU.add),
                                     reads=[PS.res[b], r_h[g]], writes=[r_h[g]])
                                PS.free(b)
                                yield
                            if gi == len(FFN_GROUPS) - 1:
                                sq_stats(junk, r_junk, 3, g)
                                finish_stats(3, g, g + 1)
                                y, ry = yb[g % 2], r_yb[g % 2]
                                if ry.w is None and not ry.r:
                                    for hr in (hs_res[2 * (g % 2)], hs_res[2 * (g % 2) + 1]):
                                        ry.w = hr.w
                                        for k_, v_ in hr.r.items():
                                            ry.r[k_] = max(ry.r.get(k_, 0), v_)
                                S.op("dve", lambda e: e.scalar_tensor_tensor(out=y, in0=h[:, g, :],
                                                                              scalar=rstd[:, 3, g:g + 1], in1=gbf[:],
                                                                              op0=ALU.mult, op1=ALU.mult),
                                     reads=[r_h[g], r_rstd[3][g], r_gbf], writes=[ry])
                                S.dma("sp", out_d[g * 128:(g + 1) * 128, :], y, reads=[ry])

                    A3s(0)
                    run(A3(0, scale=False))
                    A3s(1)
                    interleave((GU3(0), 1), (A3(1, scale=False), 1))
                    for i in range(len(steps)):
                        gi, t = steps[i]
                        nxt = GU3(i + 1) if i + 1 < len(steps) else None
                        extra = None
                        if gi == 0 and t + 2 < NT3:
                            A3s(t + 2)
                            extra = A3(t + 2, scale=False)
                        interleave((DN3(i), 1), (nxt, 1), (extra, 1))
                        if t == NT3 - 1 and gi + 2 < len(FFN_GROUPS):
                            load_group(gi + 2)
                    S.barrier()
    return nc


def make_in_maps(inputs):
    f = lambda a: np.ascontiguousarray(np.asarray(a, dtype=np.float32))
    x = f(inputs["x"])
    mem = f(inputs["mem"])
    B, SEQ, _ = x.shape
    per_b = SEQ // TOK
    assert B * per_b == NCORES
    b_in = f(inputs["b_in"])

    def pp(v):
        return np.ascontiguousarray(v.reshape(-1, 128).T)

    vec_base = np.concatenate([pp(b_in), pp(f(inputs["conv_b"])), pp(f(inputs["conv_ln_g"])),
                               pp(f(inputs["conv_ln_b"])), pp(f(inputs["gm_ln_g"])), pp(f(inputs["gm_ln_b"]))],
                              axis=1)
    conv_w = f(inputs["conv_w"])
    cw_pad = np.zeros((32, 512), np.float32)
    cw_pad[:KCONV] = conv_w
    cwp = np.ascontiguousarray(cw_pad.reshape(8, 4, 16, 32).transpose(1, 3, 2, 0).reshape(128, 16, 8))
    e32 = np.ascontiguousarray(np.tile(np.eye(32, dtype=np.float32), (4, 1)))
    w_s = f(inputs["gm_w_s"])
    wst = np.ascontiguousarray(w_s.transpose(2, 0, 1))
    sidx = np.arange(128)
    maskT = np.ascontiguousarray(
        np.broadcast_to((sidx[:, None] <= sidx[None, :]).astype(np.float32)[:, None, :], (128, 8, 128)))
    b_s = f(inputs["gm_b_s"])
    bs_rows = np.repeat(b_s, 64, axis=0).reshape(4, 128, 128).transpose(1, 0, 2)
    bs_rep = np.ascontiguousarray(bs_rows)
    ident = np.eye(128, dtype=np.float32)
    shared = {k: f(inputs[k]) for k in ("w_in", "w_out", "xa_wq", "xa_wkv", "xa_wo", "ffn_w_gate_up", "ffn_w_down",
                                        "norm_mix_g", "norm_xa_g", "norm_ffn_g", "final_norm_g", "mem_norm_g")}
    shared.update(bv_nat=np.ascontiguousarray(b_in[1536:2048]), gm_g_nat=f(inputs["gm_ln_g"]),
                  gm_b_nat=f(inputs["gm_ln_b"]))
    shared.update(cwp=cwp, E32=e32, wst=wst, maskT=maskT, bs_rep=bs_rep, ident=ident)
    in_maps = []
    for core in range(NCORES):
        b, part = divmod(core, per_b)
        t0 = part * TOK
        xs = x[b, t0:t0 + TOK]
        if t0 == 0:
            xh = np.zeros((128, D), np.float32)
            flag = 0.0
        else:
            xh = x[b, t0 - 128:t0]
            flag = 1.0
        vec = np.zeros((128, 40), np.float32)
        vec[:, :36] = vec_base
        vec[:, 36] = flag
        m = dict(shared)
        m.update(x=np.ascontiguousarray(xs), xh=np.ascontiguousarray(xh), mem=mem[b], vec=vec)
        in_maps.append(m)
    return in_maps, (B, SEQ, per_b)


def kernel(**inputs):
    in_maps, (B, SEQ, per_b) = make_in_maps(inputs)
    nc = build_nc()
    res = run_bass_kernel_spmd(nc, in_maps, core_ids=list(range(NCORES)))
    out = np.empty((B, SEQ, D), np.float32)
    for core in range(NCORES):
        b, part = divmod(core, per_b)
        out[b, part * TOK:(part + 1) * TOK] = res.results[core]["out"]
    return out
```

```python
import numpy as np
from contextlib import ExitStack
import concourse.bass as bass
import concourse.mybir as mybir
from concourse.bass_utils import run_bass_kernel_spmd

F32 = mybir.dt.float32
BF16 = mybir.dt.bfloat16
AF = mybir.ActivationFunctionType
ALU = mybir.AluOpType

NCORES = 8
D = 1024
TOK = 2048
NSUB = TOK // 128
MEM = 256
HID = 2816
NHC = HID // 128
FFN_GROUPS = [(0, 6), (6, 6), (12, 5), (17, 5)]
GMAX = 6
TS1 = 256
NT1 = TOK // TS1
KCONV = 31
RMS_EPS = 1e-6
LN_EPS = 1e-5


class Res:
    __slots__ = ("name", "w", "r")

    def __init__(self, name):
        self.name = name
        self.w = None
        self.r = {}


class Sched:
    NDS = 16

    def __init__(self, nc, es):
        self.nc = nc
        self.eng = dict(pe=nc.tensor, act=nc.scalar, dve=nc.vector, pool=nc.gpsimd, sp=nc.sync)
        self.sems = {}
        for k in self.eng:
            self.sems[k] = es.enter_context(nc.semaphore("s_" + k))
        self.cnt = {k: 0 for k in self.eng}
        self.seen = {k: {} for k in self.eng}
        self.logging = False
        self.log = {k: [] for k in self.eng}
        self.dq = {}
        for q in ("sp", "pool"):
            lst = []
            for i in range(self.NDS):
                key = "d_%s%d" % (q, i)
                self.sems[key] = es.enter_context(nc.semaphore(key))
                self.cnt[key] = 0
                lst.append(key)
            self.dq[q] = [lst, 0]

    def _where(self):
        import sys
        f = sys._getframe(2)
        names = []
        while f is not None and len(names) < 6:
            n = f.f_code.co_name
            if n == "build_nc":
                break
            if n not in ("<lambda>", "op", "mm_group", "run", "interleave", "step", "chain"):
                loc = f.f_locals
                tag = n
                for v in ("t", "i", "c", "s", "hd", "dc"):
                    if v in loc and isinstance(loc[v], int):
                        tag += " %s=%d" % (v, loc[v])
                names.append(tag)
            f = f.f_back
        return " < ".join(names)

    def _wait(self, e, deps):
        for key, val in sorted(deps):
            if key == e and e == "pe":
                continue
            if self.seen[e].get(key, 0) >= val:
                continue
            self.eng[e].wait_ge(self.sems[key], val)
            self.seen[e][key] = val

    def _deps(self, reads, writes):
        deps = set()
        for r in reads:
            if r.w is not None:
                deps.add(r.w)
        for w in writes:
            if w.w is not None:
                deps.add(w.w)
            for k, v in w.r.items():
                deps.add((k, v))
        return deps

    def _mark(self, mark, reads, writes):
        k, v = mark
        for r in reads:
            if r.r.get(k, 0) < v:
                r.r[k] = v
        for w in writes:
            w.w = mark
            w.r = {}

    def op(self, e, fn, reads=(), writes=(), signal=True):
        self._wait(e, self._deps(reads, writes))
        inst = fn(self.eng[e])
        if self.logging:
            self.log[e].append(self._where())
        if signal:
            self.cnt[e] += 1
            inst.then_inc(self.sems[e], 1)
            mark = (e, self.cnt[e])
        else:
            assert e == "pe"
            mark = (e, self.cnt[e] + 1)
        self._mark(mark, reads, writes)
        return inst

    def dma(self, q, out, in_, reads=(), writes=(), **kw):
        lst, nxt = self.dq[q]
        key = lst[nxt % self.NDS]
        self.dq[q][1] = nxt + 1
        deps = self._deps(reads, writes)
        if self.cnt[key] > 0:
            deps.add((key, self.cnt[key]))
        self._wait(q, deps)
        inst = self.eng[q].dma_start(out=out, in_=in_, **kw)
        self.cnt[key] += 16
        inst.then_inc(self.sems[key], 16)
        self._mark((key, self.cnt[key]), reads, writes)
        return inst

    def wait_all(self, e):
        deps = set()
        for k, v in self.cnt.items():
            if v > 0:
                deps.add((k, v))
        self._wait(e, deps)

    def barrier(self, dma=True):
        for e in self.eng:
            if dma:
                self.wait_all(e)
            else:
                self._wait(e, {(k, self.cnt[k]) for k in self.eng if self.cnt[k] > 0})


class PsPool:
    def __init__(self, nc, es, n=8):
        self.n = n
        self.t = [es.enter_context(nc.psum_tensor("psb%d" % i, [128, 512], F32)) for i in range(n)]
        self.res = [Res("psb%d" % i) for i in range(n)]
        self.held = [False] * n
        self.freeq = list(range(n))

    def get(self):
        if not self.freeq:
            raise RuntimeError("PSUM banks exhausted")
        i = self.freeq.pop(0)
        self.held[i] = True
        return i

    def free(self, i):
        assert self.held[i]
        self.held[i] = False
        self.freeq.append(i)


LOGGING = False


def run(gen):
    for _ in gen:
        pass


def interleave(*items):
    active = [[g, n] for g, n in items if g is not None]
    while active:
        for it in list(active):
            for _ in range(it[1]):
                try:
                    next(it[0])
                except StopIteration:
                    active.remove(it)
                    break


def step(gen, n):
    if gen is None:
        return
    for _ in range(n):
        try:
            next(gen)
        except StopIteration:
            return


def chain(*gens):
    for g in gens:
        for _ in g:
            yield


def build_nc():
    nc = bass.Bass("TRN2", target_bir_lowering=False)

    def din(name, shape):
        return nc.dram_tensor(name, list(shape), F32, kind="ExternalInput").ap()

    x_d = din("x", [TOK, D])
    xh_d = din("xh", [128, D])
    mem_d = din("mem", [MEM, D])
    w_in_d = din("w_in", [D, 2048])
    w_out_d = din("w_out", [D, D])
    wq_d = din("xa_wq", [D, D])
    wkv_d = din("xa_wkv", [D, 2 * D])
    wo_d = din("xa_wo", [D, D])
    wgu_d = din("ffn_w_gate_up", [D, 2 * HID])
    wd_d = din("ffn_w_down", [HID, D])
    g_mix_d = din("norm_mix_g", [D])
    g_xa_d = din("norm_xa_g", [D])
    g_ffn_d = din("norm_ffn_g", [D])
    g_fin_d = din("final_norm_g", [D])
    g_mem_d = din("mem_norm_g", [D])
    vec_d = din("vec", [128, 40])
    cwp_d = din("cwp", [128, 16, 8])
    e32_d = din("E32", [128, 32])
    wst_d = din("wst", [128, 8, 128])
    mask_d = din("maskT", [128, 8, 128])
    bs_d = din("bs_rep", [128, 4, 128])
    ident_d = din("ident", [128, 128])
    bv_d = din("bv_nat", [512])
    gg_d = din("gm_g_nat", [512])
    gbn_d = din("gm_b_nat", [512])
    out_d = nc.dram_tensor("out", [TOK, D], F32, kind="ExternalOutput").ap()

    es = ExitStack()
    with es:
        S = Sched(nc, es)
        S.logging = LOGGING
        nc._sched = S
        PS = PsPool(nc, es)

        def T(scope, name, shape, dt):
            return scope.enter_context(nc.sbuf_tensor("sb_" + name, list(shape), dt))

        def psf(b, n=512):
            return PS.t[b][:, 0:n]

        def psh(b, half, n):
            return PS.t[b][:, half * 256:half * 256 + n]

        def psb16(b):
            return PS.t[b][:].bitcast(BF16)

        def mm_group(out_ap, out_res, pairs, reads):
            n = len(pairs)
            for i, (l, r) in enumerate(pairs):
                S.op("pe", lambda e: e.matmul(out_ap, lhsT=l, rhs=r, start=(i == 0), stop=(i == n - 1)),
                     reads=reads, writes=[out_res], signal=(i == n - 1))

        h = T(es, "h", [128, NSUB, D], F32)
        r_h = [Res("h%d" % g) for g in range(NSUB)]
        identf = T(es, "identf", [128, 128], F32)
        identb = T(es, "identb", [128, 128], BF16)
        ones_b = T(es, "ones_b", [128, 128], BF16)
        vec = T(es, "vec", [128, 40], F32)
        stat = T(es, "stat", [128, 4, NSUB], F32)
        rstd = T(es, "rstd", [128, 4, NSUB], F32)
        stath = T(es, "stath", [128, 4], F32)
        r_ident, r_vec, r_ones = Res("ident"), Res("vec"), Res("ones")
        r_stat = [[Res("stat%d_%d" % (i, g)) for g in range(NSUB)] for i in range(4)]
        r_rstd = [[Res("rstd%d_%d" % (i, g)) for g in range(NSUB)] for i in range(4)]
        r_stath = Res("stath")

        S.dma("sp", identf[:], ident_d, writes=[r_ident])
        S.dma("sp", vec[:], vec_d, writes=[r_vec])
        S.op("dve", lambda e: e.tensor_copy(out=identb[:], in_=identf[:]), reads=[r_ident], writes=[r_ident])
        S.op("pool", lambda e: e.memset(ones_b[:], 1.0), writes=[r_ones])

        def bias(i):
            return vec[:, i:i + 1]

        def nt_scale(src_ap, src_res, rstd_ap, rstd_res, gb, r_gb, hs_bufs, hs_res, idx):
            hs = hs_bufs[idx % len(hs_bufs)]
            rh = hs_res[idx % len(hs_bufs)]
            S.op("dve", lambda e: e.scalar_tensor_tensor(out=hs[:], in0=src_ap, scalar=rstd_ap, in1=gb[:],
                                                          op0=ALU.mult, op1=ALU.mult),
                 reads=[src_res, rstd_res, r_gb], writes=[rh])

        def norm_transpose(src_ap, src_res, rstd_ap, rstd_res, gb, r_gb, hs_bufs, hs_res, idx, dst_ap, dst_res,
                           scale=True, evac="act"):
            hs = hs_bufs[idx % len(hs_bufs)]
            rh = hs_res[idx % len(hs_bufs)]
            if scale:
                nt_scale(src_ap, src_res, rstd_ap, rstd_res, gb, r_gb, hs_bufs, hs_res, idx)
            b = PS.get()
            pv = psb16(b)
            for c in range(8):
                S.op("pe", lambda e: e.transpose(out=pv[:, c * 128:(c + 1) * 128], in_=hs[:, c * 128:(c + 1) * 128],
                                                 identity=identb[:]),
                     reads=[rh, r_ident], writes=[PS.res[b]], signal=(c == 7))
            if evac == "act":
                S.op("act", lambda e: e.activation(out=dst_ap, in_=pv.rearrange("p (c t) -> p c t", c=8),
                                                   func=AF.Copy), reads=[PS.res[b]], writes=[dst_res])
            else:
                S.op("dve", lambda e: e.tensor_copy(out=dst_ap.bitcast(F32),
                                                    in_=psf(b).rearrange("p (c t) -> p c t", c=8)),
                     reads=[PS.res[b]], writes=[dst_res])
            PS.free(b)

        def finish_stats(si, lo, hi):
            rs = [r_stat[si][g] for g in range(lo, hi)]
            ws = [r_rstd[si][g] for g in range(lo, hi)]
            S.op("act", lambda e: e.activation(out=rstd[:, si, lo:hi], in_=stat[:, si, lo:hi], func=AF.Ln,
                                               scale=1.0 / D, bias=RMS_EPS), reads=rs, writes=ws)
            S.op("act", lambda e: e.activation(out=rstd[:, si, lo:hi], in_=rstd[:, si, lo:hi], func=AF.Exp,
                                               scale=-0.5), reads=ws, writes=ws)

        def sq_stats(junk, r_junk, si, g, eng="act"):
            if eng == "act":
                S.op("act", lambda e: e.activation(out=junk[:], in_=h[:, g, :], func=AF.Square,
                                                   accum_out=stat[:, si, g:g + 1]),
                     reads=[r_h[g]], writes=[r_junk, r_stat[si][g]])
            else:
                S.op("dve", lambda e: e.scalar_tensor_tensor(out=junk[:], in0=h[:, g, :], scalar=1.0, in1=h[:, g, :],
                                                              op0=ALU.mult, op1=ALU.mult,
                                                              accum_out=stat[:, si, g:g + 1]),
                     reads=[r_h[g]], writes=[r_junk, r_stat[si][g]])

        s1 = ExitStack()
        with s1:
            w_in = T(s1, "w_in", [128, 8, 2048], BF16)
            w_out = T(s1, "w_out", [128, 8, D], BF16)
            r_wout = Res("w_out")
            r_winb = [Res("w_in%d" % i) for i in range(4)]
            w_in_v = w_in_d.rearrange("(k p) n -> p k n", p=128)
            for blk in (1, 0, 2, 3):
                S.dma("pool", w_in[:, :, blk * 512:(blk + 1) * 512], w_in_v[:, :, blk * 512:(blk + 1) * 512],
                      writes=[r_winb[blk]])
            win_mark = r_winb[3].w
            gb = T(s1, "gb1", [128, D], F32)
            r_gb = Res("gb1")
            cwp = T(s1, "cwp", [128, 16, 8], F32)
            e32 = T(s1, "e32", [128, 32], F32)
            r_cw = Res("cwp")
            r_e32 = Res("e32")
            wsT = T(s1, "wsT", [128, 8, 128], BF16)
            r_wsT = Res("wsT")
            bs = T(s1, "bs", [128, 4, 128], F32)
            r_bs = Res("bs")
            S.dma("sp", gb[:], g_mix_d.partition_broadcast(128), writes=[r_gb])
            junk = T(s1, "junk1", [128, D], BF16)
            r_junk = Res("junk1")
            hs_bufs = [T(s1, "hs1_%d" % i, [128, D], BF16) for i in range(2)]
            hs_res = [Res("hs1_%d" % i) for i in range(2)]
            hnTh = T(s1, "hnTh", [128, 8, 128], BF16)
            r_hnTh = Res("hnTh")
            for g in range(2):
                S.dma("sp", h[:, g, :], x_d[g * 128:(g + 1) * 128, :], writes=[r_h[g]])
            s1s = ExitStack()
            with s1s:
                xhalo = T(s1s, "xhalo", [128, D], F32)
                r_xhalo = Res("xhalo")
                S.dma("sp", xhalo[:], xh_d, writes=[r_xhalo])
                S.dma("sp", cwp[:], cwp_d, writes=[r_cw])
                S.dma("sp", e32[:], e32_d, writes=[r_e32])
                S.dma("sp", bs[:], bs_d, writes=[r_bs])
                wst_f = T(s1s, "wst_f", [128, 8, 128], F32)
                mask_f = T(s1s, "mask_f", [128, 8, 128], F32)
                r_wstf, r_maskf = Res("wstf"), Res("maskf")
                S.dma("sp", wst_f[:], wst_d, writes=[r_wstf])
                S.dma("sp", mask_f[:], mask_d, writes=[r_maskf])
                S.op("dve", lambda e: e.tensor_tensor(out=wsT[:], in0=wst_f[:], in1=mask_f[:], op=ALU.mult),
                     reads=[r_wstf, r_maskf], writes=[r_wsT])
                S.op("act", lambda e: e.activation(out=junk[:], in_=xhalo[:], func=AF.Square,
                                                   accum_out=stath[:, 0:1]),
                     reads=[r_xhalo], writes=[r_junk, r_stath])
                S.op("act", lambda e: e.activation(out=stath[:, 1:2], in_=stath[:, 0:1], func=AF.Ln, scale=1.0 / D,
                                                   bias=RMS_EPS), reads=[r_stath], writes=[r_stath])
                S.op("act", lambda e: e.activation(out=stath[:, 1:2], in_=stath[:, 1:2], func=AF.Exp, scale=-0.5),
                     reads=[r_stath], writes=[r_stath])
                norm_transpose(xhalo[:], r_xhalo, stath[:, 1:2], r_stath, gb, r_gb, hs_bufs, hs_res, 0, hnTh[:],
                               r_hnTh)
                for e_ in S.eng:
                    S._wait(e_, {("dve", S.cnt["dve"]), ("act", S.cnt["act"])})
            S._wait("sp", {r_winb[0].w})
            for g in range(2, 4):
                S.dma("sp", h[:, g, :], x_d[g * 128:(g + 1) * 128, :], writes=[r_h[g]])
            S.dma("pool", w_out[:], w_out_d.rearrange("(k p) n -> p k n", p=128), writes=[r_wout])

            WP = T(s1, "WP", [128, 16, 8, 32], BF16)
            r_WP = Res("WP")
            S.op("pool", lambda e: e.tensor_tensor(
                out=WP[:].rearrange("p q j c -> p (q j) c"),
                in0=e32[:].unsqueeze(1).broadcast_to([128, 128, 32]),
                in1=cwp[:].rearrange("p q j -> p (q j)").unsqueeze(2).broadcast_to([128, 128, 32]), op=ALU.mult),
                 reads=[r_cw, r_e32], writes=[r_WP])
            WX = 128 + TS1
            WC = WX - 2
            AXT = T(s1, "AXT", [128, 16, WX], BF16)
            r_AXT = [[Res("AXT%d_%d" % (qq, jj)) for jj in range(4)] for qq in range(4)]

            def relayout(axc, raxc):
                for qq in range(4):
                    for jj in range(4):
                        dst = AXT[jj * 32:(jj + 1) * 32, :, 0:WC].rearrange("p (c q) w -> p c q w", q=4)[:, :, qq, :]
                        S.dma("sp", dst, axc[qq * 32:(qq + 1) * 32, :, jj:jj + WC], reads=list(raxc),
                              writes=[r_AXT[qq][jj]])

            hnT = [T(s1, "hnT1_0", [128, 8, TS1], BF16)] * 2
            r_hnT = [Res("hnT1_0")] * 2
            ax = [T(s1, "ax%d" % i, [128, 4, WX + 2], BF16) for i in range(2)]
            r_ax = [[Res("ax%d_%d" % (i, c)) for c in range(4)] for i in range(2)]
            for i in range(2):
                S.op("pool", lambda e: e.memset(ax[i][:], 0.0), writes=r_ax[i])
            sig = [T(s1, "sig%d" % i, [128, TS1], BF16) for i in range(2)]
            r_sig = [Res("sig%d" % i) for i in range(2)]
            u_t = [T(s1, "u_t%d" % i, [128, 4, TS1], BF16) for i in range(2)]
            r_u = [Res("u%d" % i) for i in range(2)]
            bvt = T(s1, "bvt", [128, 512], F32)
            ggt = T(s1, "ggt", [128, 512], F32)
            gbt = T(s1, "gbt", [128, 512], F32)
            r_bvt, r_ggt, r_gbt = Res("bvt"), Res("ggt"), Res("gbt")
            S.dma("sp", bvt[:], bv_d.partition_broadcast(128), writes=[r_bvt])
            S.dma("sp", ggt[:], gg_d.partition_broadcast(128), writes=[r_ggt])
            S.dma("sp", gbt[:], gbn_d.partition_broadcast(128), writes=[r_gbt])
            zb = [T(s1, "zb%d" % i, [128, 512], F32) for i in range(2)]
            r_zb = [Res("zb%d" % i) for i in range(2)]
            vst = T(s1, "vst", [128, 8], F32)
            r_vst = Res("vst")
            By1 = T(s1, "By1", [128, 4, TS1], BF16)
            By2 = T(s1, "By2", [128, 4, TS1], BF16)
            r_By1 = [Res("By1_c%d" % c) for c in range(4)]
            r_By2 = [Res("By2_c%d" % c) for c in range(4)]
            vtm = [T(s1, "vtm%d" % i, [128, 512], BF16) for i in range(2)]
            r_vtm = [Res("vtm%d" % i) for i in range(2)]
            mixT = T(s1, "mixT", [128, 8, TS1], BF16)
            r_mixc, r_mixg = Res("mixT_conv"), Res("mixT_gm")
            lnb = {}
            for nm in ("E",):
                lnb[nm] = (T(s1, "ln_mean" + nm, [128, TS1], F32), T(s1, "ln_rstd" + nm, [128, TS1], F32),
                           Res("ln_mean" + nm), Res("ln_rstd" + nm))
            t1 = [T(s1, "t1_%d" % i, [128, TS1], F32) for i in range(2)]
            r_t1 = [Res("t1_%d" % i) for i in range(2)]
            gmt_all = T(s1, "gmt", [128, 2 * TS1], F32)
            gmt = gmt_all[:, 0:TS1]
            gmt_junk = gmt_all[:].bitcast(BF16)
            r_gmt = Res("gmt")
            nsub = TS1 // 128

            def stats1(t):
                for s in range(nsub):
                    sq_stats(junk, r_junk, 0, t * nsub + s)
                finish_stats(0, t * nsub, (t + 1) * nsub)

            stats1(0)
            def glu_chunk(c, hn_ap, r_hn, n, dst_list, maskit=False):
                sg = sig[c % 2]
                rs = r_sig[c % 2]
                b = PS.get()
                mm_group(psf(b, n), PS.res[b],
                         [(w_in[:, k, (4 + c) * 128:(5 + c) * 128], hn_ap(k)) for k in range(8)], [r_winb[1], r_hn])
                S.op("act", lambda e: e.activation(out=sg[:, 0:n], in_=psf(b, n), func=AF.Sigmoid, bias=bias(4 + c)),
                     reads=[PS.res[b], r_vec], writes=[rs])
                PS.free(b)
                b = PS.get()
                mm_group(psf(b, n), PS.res[b],
                         [(w_in[:, k, c * 128:(c + 1) * 128], hn_ap(k)) for k in range(8)], [r_winb[0], r_hn])
                for (dst, rd, lo, hi) in dst_list:
                    S.op("dve", lambda e: e.scalar_tensor_tensor(out=dst, in0=PS.t[b][:, lo:hi], scalar=bias(c),
                                                                  in1=sg[:, lo:hi], op0=ALU.add, op1=ALU.mult),
                         reads=[PS.res[b], rs, r_vec], writes=[rd])
                    if maskit:
                        S.op("dve", lambda e: e.tensor_scalar(out=dst, in0=dst, scalar1=vec[:, 36:37], scalar2=None,
                                                              op0=ALU.mult), reads=[rd, r_vec], writes=[rd])
                PS.free(b)

            def A1s(t):
                for s in range(nsub):
                    g = t * nsub + s
                    nt_scale(h[:, g, :], r_h[g], rstd[:, 0, g:g + 1], r_rstd[0][g], gb, r_gb, hs_bufs, hs_res, g)

            def A1(t, scale=True):
                hb, rhb = hnT[t % 2], r_hnT[t % 2]
                for s in range(nsub):
                    g = t * nsub + s
                    norm_transpose(h[:, g, :], r_h[g], rstd[:, 0, g:g + 1], r_rstd[0][g], gb, r_gb, hs_bufs, hs_res,
                                   g, hb[:, :, s * 128:(s + 1) * 128], rhb, scale=scale, evac="dve")
                    yield

            def B1(t):
                hb, rhb = hnT[t % 2], r_hnT[t % 2]
                axc, raxc = ax[t % 2], r_ax[t % 2]
                axn, raxn = ax[(t + 1) % 2], r_ax[(t + 1) % 2]
                hn_ap = lambda k: hb[:, k, :]
                for c in range(4):
                    dsts = [(axc[:, c, 128:128 + TS1], raxc[c], 0, TS1)]
                    if t + 1 < NT1:
                        dsts.append((axn[:, c, 0:128], raxn[c], TS1 - 128, TS1))
                    glu_chunk(c, hn_ap, rhb, TS1, dsts)
                    if c == 3:
                        relayout(axc, raxc)
                    yield
                for c in range(4):
                    b = PS.get()
                    mm_group(psf(b, TS1), PS.res[b],
                             [(w_in[:, k, (8 + c) * 128:(9 + c) * 128], hn_ap(k)) for k in range(8)],
                             [r_winb[2], rhb])
                    S.op("act", lambda e: e.activation(out=u_t[t % 2][:, c, :], in_=psf(b, TS1),
                                                       func=AF.Gelu_apprx_tanh, bias=bias(8 + c)),
                         reads=[PS.res[b], r_vec], writes=[r_u[t % 2]])
                    PS.free(b)
                    yield
                for s in range(nsub):
                    b = PS.get()
                    mm_group(psf(b), PS.res[b],
                             [(hb[:, k, s * 128:(s + 1) * 128], w_in[:, k, 1536:2048]) for k in range(8)],
                             [r_winb[3], rhb])
                    S.op("dve", lambda e: e.tensor_tensor(out=zb[s][:], in0=psf(b), in1=bvt[:], op=ALU.add),
                         reads=[PS.res[b], r_bvt], writes=[r_zb[s]])
                    PS.free(b)
                    S.op("act", lambda e: e.activation(out=zb[s][:], in_=zb[s][:], func=AF.Gelu_apprx_tanh,
                                                       accum_out=vst[:, s:s + 1]),
                         reads=[r_zb[s]], writes=[r_zb[s], r_vst])
                    S.op("act", lambda e: e.activation(out=junk[:, 0:512], in_=zb[s][:], func=AF.Square,
                                                       accum_out=vst[:, 2 + s:3 + s]),
                         reads=[r_zb[s]], writes=[r_junk, r_vst])
                    yield
                S.op("dve", lambda e: e.tensor_scalar(out=vst[:, 4:6], in0=vst[:, 0:2], scalar1=1.0 / 512, scalar2=None,
                                                      op0=ALU.mult), reads=[r_vst], writes=[r_vst])
                S.op("dve", lambda e: e.tensor_tensor(out=vst[:, 6:8], in0=vst[:, 4:6], in1=vst[:, 4:6], op=ALU.mult),
                     reads=[r_vst], writes=[r_vst])
                S.op("dve", lambda e: e.scalar_tensor_tensor(out=vst[:, 6:8], in0=vst[:, 2:4], scalar=1.0 / 512,
                                                              in1=vst[:, 6:8], op0=ALU.mult, op1=ALU.subtract),
                     reads=[r_vst], writes=[r_vst])
                S.op("act", lambda e: e.activation(out=vst[:, 6:8], in_=vst[:, 6:8], func=AF.Ln, bias=LN_EPS),
                     reads=[r_vst], writes=[r_vst])
                S.op("act", lambda e: e.activation(out=vst[:, 6:8], in_=vst[:, 6:8], func=AF.Exp, scale=-0.5),
                     reads=[r_vst], writes=[r_vst])
                for s in range(nsub):
                    S.op("dve", lambda e: e.scalar_tensor_tensor(out=zb[s][:], in0=zb[s][:], scalar=vst[:, 4 + s:5 + s],
                                                                  in1=ggt[:], op0=ALU.subtract, op1=ALU.mult),
                         reads=[r_zb[s], r_vst, r_ggt], writes=[r_zb[s]])
                    S.op("dve", lambda e: e.scalar_tensor_tensor(out=vtm[s][:], in0=zb[s][:], scalar=vst[:, 6 + s:7 + s],
                                                                  in1=gbt[:], op0=ALU.mult, op1=ALU.add),
                         reads=[r_zb[s], r_vst, r_gbt], writes=[r_vtm[s]])
                yield

            def ln_stats(nm, src, r_src, srcsq, r_sq):
                mean, rs_, r_m, r_r = lnb[nm]
                b1 = PS.get()
                mm_group(psf(b1, TS1), PS.res[b1], [(ones_b[:], src[:, c, :]) for c in range(4)], [r_ones] + list(r_src))
                b2 = PS.get()
                mm_group(psf(b2, TS1), PS.res[b2], [(ones_b[:], srcsq[:, c, :]) for c in range(4)], [r_ones] + list(r_sq))
                S.op("act", lambda e: e.activation(out=mean[:], in_=psf(b1, TS1), func=AF.Copy, scale=1.0 / 512),
                     reads=[PS.res[b1]], writes=[r_m])
                S.op("act", lambda e: e.activation(out=rs_[:], in_=psf(b1, TS1), func=AF.Square, scale=1.0 / 512),
                     reads=[PS.res[b1]], writes=[r_r])
                PS.free(b1)
                S.op("dve", lambda e: e.scalar_tensor_tensor(out=rs_[:], in0=psf(b2, TS1), scalar=1.0 / 512,
                                                              in1=rs_[:], op0=ALU.mult, op1=ALU.subtract),
                     reads=[PS.res[b2], r_r], writes=[r_r])
                PS.free(b2)
                S.op("act", lambda e: e.activation(out=rs_[:], in_=rs_[:], func=AF.Ln, bias=LN_EPS),
                     reads=[r_r], writes=[r_r])
                S.op("act", lambda e: e.activation(out=rs_[:], in_=rs_[:], func=AF.Exp, scale=-0.5),
                     reads=[r_r], writes=[r_r])

            def ln_apply(nm, src, r_src, func, gi, bi, dst_fn, r_dst):
                mean, rs_, r_m, r_r = lnb[nm]
                for c in range(4):
                    tt = t1[c % 2]
                    rt = r_t1[c % 2]
                    S.op("dve", lambda e: e.tensor_tensor(out=tt[:], in0=src[:, c, :], in1=mean[:],
                                                          op=ALU.subtract), reads=[r_src[c], r_m], writes=[rt])
                    S.op("dve", lambda e: e.tensor_tensor(out=tt[:], in0=tt[:], in1=rs_[:], op=ALU.mult),
                         reads=[rt, r_r], writes=[rt])
                    S.op("act", lambda e: e.activation(out=dst_fn(c), in_=tt[:], func=func, scale=bias(gi + c),
                                                       bias=bias(bi + c)), reads=[rt, r_vec], writes=[r_dst])
                    yield

            def C1(t):
                axc, raxc = ax[t % 2], r_ax[t % 2]
                for c in range(4):
                    b = PS.get()
                    for J in range(8):
                        for qq in range(4):
                            q = 4 * c + qq
                            S.op("pe", lambda e: e.matmul(PS.t[b][32 * qq:32 * qq + 32, 0:TS1], lhsT=WP[:, q, J, :],
                                                          rhs=AXT[:, q, 98 + 4 * J:98 + 4 * J + TS1],
                                                          start=(J == 0), stop=(J == 7), tile_position=(0, 32 * qq)),
                                 reads=[r_WP] + r_AXT[qq], writes=[PS.res[b]], signal=(J == 7 and qq == 3))
                    S.op("act", lambda e: e.activation(out=By1[:, c, :], in_=psf(b, TS1), func=AF.Identity,
                                                       bias=bias(16 + c)), reads=[PS.res[b], r_vec], writes=[r_By1[c]])
                    PS.free(b)
                    S.op("pool", lambda e: e.tensor_tensor(out=By2[:, c, :], in0=By1[:, c, :], in1=By1[:, c, :],
                                                          op=ALU.mult), reads=[r_By1[c]], writes=[r_By2[c]])
                    yield

            def F1(t):
                u_c, r_uc = u_t[t % 2], r_u[t % 2]
                bm2 = [PS.get() for _ in range(2)]
                bm = [bm2[0], bm2[0], bm2[1], bm2[1]]
                for s in range(nsub):
                    vt, rvt = vtm[s % 2], r_vtm[s % 2]
                    for c in range(4):
                        for hh in range(2):
                            hd = 2 * c + hh
                            last = (s == nsub - 1) and (hh == 1)
                            S.op("pe", lambda e: e.matmul(PS.t[bm[c]][hh * 64:(hh + 1) * 64,
                                                                       (c % 2) * 256 + s * 128:(c % 2) * 256 + (s + 1) * 128],
                                                          lhsT=vt[:, hd * 64:(hd + 1) * 64], rhs=wsT[:, hd, :],
                                                          start=True, stop=True),
                                 reads=[rvt, r_wsT], writes=[PS.res[bm[c]]], signal=last)
                    yield
                for c in range(4):
                    S.op("dve", lambda e: e.tensor_tensor(
                        out=gmt[:].rearrange("p (s t) -> p s t", s=nsub),
                        in0=psh(bm[c], c % 2, TS1).rearrange("p (s t) -> p s t", s=nsub),
                        in1=bs[:, c, :].unsqueeze(1).broadcast_to([128, nsub, 128]), op=ALU.add),
                         reads=[PS.res[bm[c]], r_bs], writes=[r_gmt])
                    S.op("dve", lambda e: e.tensor_tensor(out=mixT[:, 4 + c, :], in0=gmt[:], in1=u_c[:, c, :],
                                                          op=ALU.mult), reads=[r_gmt, r_uc], writes=[r_mixg])
                    if c % 2 == 1:
                        PS.free(bm[c])
                    yield

            def G1(t):
                for s in range(nsub):
                    g = t * nsub + s
                    for hf in range(2):
                        b = PS.get()
                        mm_group(psf(b), PS.res[b],
                                 [(mixT[:, k, s * 128:(s + 1) * 128], w_out[:, k, hf * 512:(hf + 1) * 512])
                                  for k in range(8)], [r_mixc, r_mixg, r_wout])
                        S.op("dve", lambda e: e.tensor_tensor(out=h[:, g, hf * 512:(hf + 1) * 512], in0=psf(b),
                                                              in1=h[:, g, hf * 512:(hf + 1) * 512], op=ALU.add),
                             reads=[PS.res[b], r_h[g]], writes=[r_h[g]])
                        PS.free(b)
                        yield
                    sq_stats(gmt_junk, r_gmt, 1, g, eng="dve")

            A1s(0)
            run(A1(0, scale=False))
            stats1(1)
            for c in range(4):
                glu_chunk(c, lambda k: hnTh[:, k, :], r_hnTh, 128, [(ax[0][:, c, 0:128], r_ax[0][c], 0, 128)],
                          maskit=True)
            gB0 = B1(0)
            step(gB0, 8)
            S._wait("sp", {win_mark})
            for g in range(4, NSUB):
                S.dma("sp", h[:, g, :], x_d[g * 128:(g + 1) * 128, :], writes=[r_h[g]])
            A1s(1)
            run(gB0)
            run(A1(1, scale=False))
            stats1(2)
            for t in range(NT1):
                if t + 2 < NT1:
                    A1s(t + 2)
                run(C1(t))
                ln_stats("E", By1, r_By1, By2, r_By2)
                gB = B1(t + 1) if t + 1 < NT1 else None
                step(gB, 4)
                gF = F1(t)
                step(gF, 1)
                step(gB, 1)
                step(gF, 1)
                step(gB, 1)
                run(ln_apply("E", By1, r_By1, AF.Silu, 20, 24, lambda c: mixT[:, c, :], r_mixc))
                step(gF, 1)
                step(gB, 1)
                run(gF)
                if gB is not None:
                    run(gB)
                if t + 3 < NT1:
                    stats1(t + 3)
                if t + 2 < NT1:
                    run(A1(t + 2, scale=False))
                run(G1(t))
            S.barrier()

        s23 = ExitStack()
        with s23:
            wg = [None, None]
            wu = [None, None]
            wd = [None, None]
            wg[0] = T(s23, "wg0", [128, 8, GMAX * 128], BF16)
            wu[0] = T(s23, "wu0", [128, 8, GMAX * 128], BF16)
            wd[0] = T(s23, "wd0", [128, GMAX, D], BF16)
            gb3 = T(s23, "gb3", [128, D], F32)
            r_gb3 = Res("gb3")
            r_wg = [Res("wg%d" % i) for i in range(2)]
            r_wu = [Res("wu%d" % i) for i in range(2)]
            r_wd = [Res("wd%d" % i) for i in range(2)]

            def load_group(gi):
                c0, G = FFN_GROUPS[gi]
                bi = gi % 2
                S.dma("pool", wg[bi][:, :, 0:G * 128],
                      wgu_d[:, c0 * 128:(c0 + G) * 128].rearrange("(k p) n -> p k n", p=128), writes=[r_wg[bi]])
                S.dma("pool", wu[bi][:, :, 0:G * 128],
                      wgu_d[:, HID + c0 * 128:HID + (c0 + G) * 128].rearrange("(k p) n -> p k n", p=128),
                      writes=[r_wu[bi]])
                S.dma("pool", wd[bi][:, 0:G, :],
                      wd_d[c0 * 128:(c0 + G) * 128, :].rearrange("(g p) n -> p g n", p=128), writes=[r_wd[bi]])

            s2 = ExitStack()
            with s2:
                wq = T(s2, "wq", [128, 8, D], BF16)
                wo = T(s2, "wo", [128, 8, D], BF16)
                KT = T(s2, "KT", [128, 8, MEM], BF16)
                Vt = T(s2, "Vt", [128, 2, D], BF16)
                r_wq, r_wo, r_KT, r_V = Res("wq"), Res("wo"), Res("KT"), Res("V")
                gb = T(s2, "gb2", [128, D], F32)
                r_gb = Res("gb2")
                hs_bufs = [T(s2, "hs2_%d" % i, [128, D], BF16) for i in range(4)]
                hs_res = [Res("hs2_%d" % i) for i in range(4)]
                junk = T(s2, "junk2", [128, D], BF16)
                r_junk = Res("junk2")
                s2a = ExitStack()
                with s2a:
                    wkv = T(s2a, "wkv", [128, 8, 2 * D], BF16)
                    r_wkh = [Res("wk%d" % i) for i in range(2)]
                    r_wvh = [Res("wv%d" % i) for i in range(2)]
                    wkv_v = wkv_d.rearrange("(k p) n -> p k n", p=128)
                    for i in range(2):
                        S.dma("pool", wkv[:, :, i * 512:(i + 1) * 512], wkv_v[:, :, i * 512:(i + 1) * 512],
                              writes=[r_wkh[i]])
                    for i in range(2):
                        S.dma("pool", wkv[:, :, D + i * 512:D + (i + 1) * 512],
                              wkv_v[:, :, D + i * 512:D + (i + 1) * 512], writes=[r_wvh[i]])
                    S.dma("pool", wq[:], wq_d.rearrange("(k p) n -> p k n", p=128), writes=[r_wq])
                    S.dma("pool", wo[:], wo_d.rearrange("(k p) n -> p k n", p=128), writes=[r_wo])
                    load_group(0)
                    S.dma("sp", gb3[:], g_ffn_d.partition_broadcast(128), writes=[r_gb3])
                    msb = T(s2a, "msb", [128, 2, D], F32)
                    r_msb = Res("msb")
                    gbm = T(s2a, "gbm", [128, D], F32)
                    r_gbm = Res("gbm")
                    mnT = T(s2a, "mnT", [128, 8, MEM], BF16)
                    r_mnT = Res("mnT")
                    S.dma("sp", msb[:], mem_d.rearrange("(c p) n -> p c n", p=128), writes=[r_msb])
                    S.dma("sp", gbm[:], g_mem_d.partition_broadcast(128), writes=[r_gbm])
                    S.dma("sp", gb[:], g_xa_d.partition_broadcast(128), writes=[r_gb])
                    for mc in range(2):
                        S.op("act", lambda e: e.activation(out=junk[:], in_=msb[:, mc, :], func=AF.Square,
                                                           accum_out=stath[:, 2 + mc:3 + mc]),
                             reads=[r_msb], writes=[r_junk, r_stath])
                    S.op("act", lambda e: e.activation(out=stath[:, 2:4], in_=stath[:, 2:4], func=AF.Ln,
                                                       scale=1.0 / D, bias=RMS_EPS), reads=[r_stath], writes=[r_stath])
                    S.op("act", lambda e: e.activation(out=stath[:, 2:4], in_=stath[:, 2:4], func=AF.Exp,
                                                       scale=-0.5), reads=[r_stath], writes=[r_stath])
                    finish_stats(1, 0, NSUB)
                    for mc in range(2):
                        norm_transpose(msb[:, mc, :], r_msb, stath[:, 2 + mc:3 + mc], r_stath, gbm, r_gbm, hs_bufs,
                                       hs_res, mc, mnT[:, :, mc * 128:(mc + 1) * 128], r_mnT)
                    for dc in range(8):
                        b = PS.get()
                        mm_group(psf(b, MEM), PS.res[b],
                                 [(wkv[:, k, dc * 128:(dc + 1) * 128], mnT[:, k, :]) for k in range(8)],
                                 [r_wkh[dc // 4], r_mnT])
                        S.op("act", lambda e: e.activation(out=KT[:, dc, :], in_=psf(b, MEM), func=AF.Copy),
                             reads=[PS.res[b]], writes=[r_KT])
                        PS.free(b)
                    for mc in range(2):
                        for hf in range(2):
                            b = PS.get()
                            mm_group(psf(b), PS.res[b],
                                     [(mnT[:, k, mc * 128:(mc + 1) * 128],
                                       wkv[:, k, D + hf * 512:D + (hf + 1) * 512]) for k in range(8)], [r_wvh[hf], r_mnT])
                            S.op("act", lambda e: e.activation(out=Vt[:, mc, hf * 512:(hf + 1) * 512], in_=psf(b),
                                                               func=AF.Copy), reads=[PS.res[b]], writes=[r_V])
                            PS.free(b)
                    S.barrier(dma=False)
                hnT = [T(s2, "hnT2_0", [128, 8, 512], BF16)] * 2
                r_hnT = [Res("hnT2_0")] * 2
                qT = [T(s2, "qT%d" % i, [128, 8, 512], BF16) for i in range(2)]
                r_qT = [[Res("qT%d_%d" % (i, dc)) for dc in range(8)] for i in range(2)]
                PT = T(s2, "PT", [128, 4, 2, 512], BF16)
                rden = T(s2, "rden", [128, 4, 512], F32)
                oT = T(s2, "oT", [128, 8, 512], BF16)
                r_PT = [Res("PT%d" % i) for i in range(4)]
                r_rden = [Res("rden%d" % i) for i in range(4)]
                r_oT = Res("oT")
                SCALE = 256 ** -0.5
                NT2 = TOK // 512

                def A2s(t):
                    for s in range(4):
                        g = t * 4 + s
                        nt_scale(h[:, g, :], r_h[g], rstd[:, 1, g:g + 1], r_rstd[1][g], gb, r_gb, hs_bufs, hs_res, g)

                def A2(t, scale=True):
                    hb, rhb = hnT[t % 2], r_hnT[t % 2]
                    for s in range(4):
                        g = t * 4 + s
                        norm_transpose(h[:, g, :], r_h[g], rstd[:, 1, g:g + 1], r_rstd[1][g], gb, r_gb, hs_bufs,
                                       hs_res, g, hb[:, :, s * 128:(s + 1) * 128], rhb, scale=scale, evac="dve")
                        yield

                def Q2(t):
                    hb, rhb = hnT[t % 2], r_hnT[t % 2]
                    for dc in range(8):
                        b = PS.get()
                        mm_group(psf(b), PS.res[b],
                                 [(wq[:, k, dc * 128:(dc + 1) * 128], hb[:, k, :]) for k in range(8)], [r_wq, rhb])
                        if dc % 2 == 0:
                            S.op("act", lambda e: e.activation(out=qT[t % 2][:, dc, :], in_=psf(b), func=AF.Copy),
                                 reads=[PS.res[b]], writes=[r_qT[t % 2][dc]])
                        else:
                            S.op("dve", lambda e: e.tensor_copy(out=qT[t % 2][:, dc, :], in_=psf(b)),
                                 reads=[PS.res[b]], writes=[r_qT[t % 2][dc]])
                        PS.free(b)
                        yield

                def SH2(t):
                    q, rq = qT[t % 2], r_qT[t % 2]

                    def sc(hd):
                        for mc in range(2):
                            b = PS.get()
                            mm_group(psf(b), PS.res[b],
                                     [(KT[:, 2 * hd + i, mc * 128:(mc + 1) * 128], q[:, 2 * hd + i, :])
                                      for i in range(2)], [r_KT, rq[2 * hd], rq[2 * hd + 1]])
                            S.op("act", lambda e: e.activation(out=PT[:, hd, mc, :], in_=psf(b), func=AF.Exp,
                                                               scale=SCALE), reads=[PS.res[b]], writes=[r_PT[hd]])
                            PS.free(b)

                    def dn(hd):
                        b = PS.get()
                        mm_group(psf(b), PS.res[b], [(ones_b[:], PT[:, hd, mc, :]) for mc in range(2)],
                                 [r_ones, r_PT[hd]])
                        S.op("act", lambda e: e.activation(out=rden[:, hd, :], in_=psf(b), func=AF.Ln),
                             reads=[PS.res[b]], writes=[r_rden[hd]])
                        PS.free(b)
                        S.op("act", lambda e: e.activation(out=rden[:, hd, :], in_=rden[:, hd, :], func=AF.Exp,
                                                           scale=-1.0), reads=[r_rden[hd]], writes=[r_rden[hd]])

                    def pv_(hd):
                        for i in range(2):
                            dc = 2 * hd + i
                            b = PS.get()
                            mm_group(psf(b), PS.res[b],
                                     [(Vt[:, mc, dc * 128:(dc + 1) * 128], PT[:, hd, mc, :]) for mc in range(2)],
                                     [r_V, r_PT[hd]])
                            S.op("dve", lambda e: e.tensor_tensor(out=oT[:, dc, :], in0=psf(b), in1=rden[:, hd, :],
                                                                  op=ALU.mult),
                                 reads=[PS.res[b], r_rden[hd]], writes=[r_oT])
                            PS.free(b)

                    seq = [(sc, 0), (sc, 1), (sc, 2), (dn, 0), (sc, 3), (dn, 1), (pv_, 0), (dn, 2), (pv_, 1),
                           (dn, 3), (pv_, 2), (pv_, 3)]
                    for fn, hd in seq:
                        fn(hd)
                        yield

                def O2(t):
                    for s in range(4):
                        g = t * 4 + s
                        for hf in range(2):
                            b = PS.get()
                            mm_group(psf(b), PS.res[b],
                                     [(oT[:, k, s * 128:(s + 1) * 128], wo[:, k, hf * 512:(hf + 1) * 512])
                                      for k in range(8)], [r_oT, r_wo])
                            S.op("dve", lambda e: e.tensor_tensor(out=h[:, g, hf * 512:(hf + 1) * 512], in0=psf(b),
                                                                  in1=h[:, g, hf * 512:(hf + 1) * 512], op=ALU.add),
                                 reads=[PS.res[b], r_h[g]], writes=[r_h[g]])
                            PS.free(b)
                            yield
                        sq_stats(junk, r_junk, 2, g)

                A2s(0)
                run(A2(0, scale=False))
                run(Q2(0))
                A2s(1)
                run(A2(1, scale=False))
                for t in range(NT2):
                    nxt = Q2(t + 1) if t + 1 < NT2 else None
                    if t + 2 < NT2:
                        A2s(t + 2)
                    interleave((SH2(t), 3), (nxt, 2))
                    run(O2(t))
                    if t + 2 < NT2:
                        run(A2(t + 2, scale=False))
                finish_stats(2, 0, NSUB)
                S.barrier()

            s3 = ExitStack()
            with s3:
                gb, r_gb = gb3, r_gb3
                gbf = T(s3, "gbf", [128, D], F32)
                r_gbf = Res("gbf")
                S.dma("sp", gbf[:], g_fin_d.partition_broadcast(128), writes=[r_gbf])
                junk = T(s3, "junk3", [128, D], BF16)
                r_junk = Res("junk3")
                hsall = T(s3, "hs3", [128, 4, D], BF16)
                hs_bufs = [hsall[:, i, :] for i in range(4)]
                hs_res = [Res("hs3_%d" % i) for i in range(4)]
                yb = [hsall[:, 2 * j:2 * j + 2, :].rearrange("p a n -> p (a n)").bitcast(F32) for j in range(2)]
                r_yb = [Res("yb%d" % j) for j in range(2)]
                s3a = ExitStack()
                with s3a:
                    wg[1] = T(s3a, "wg1", [128, 8, GMAX * 128], BF16)
                    wu[1] = T(s3a, "wu1", [128, 8, GMAX * 128], BF16)
                    wd[1] = T(s3a, "wd1", [128, GMAX, D], BF16)
                    load_group(1)
                    hnT3 = T(s3a, "hnT3", [128, 8, TOK], BF16)
                    r_hnT3 = [Res("hnT3_%d" % g) for g in range(NSUB)]
                    sgb = [T(s3a, "sg%d" % i, [128, 512], F32) for i in range(2)]
                    r_sg = [Res("sg%d" % i) for i in range(2)]
                    hmid = [T(s3a, "hmid%d" % i, [128, GMAX, 512], BF16) for i in range(2)]
                    r_hmid = [Res("hmid%d" % i) for i in range(2)]
                    NT3 = TOK // 512

                    def A3s(t):
                        for s in range(4):
                            g = t * 4 + s
                            nt_scale(h[:, g, :], r_h[g], rstd[:, 2, g:g + 1], r_rstd[2][g], gb, r_gb, hs_bufs,
                                     hs_res, g)

                    def A3(t, scale=True):
                        for s in range(4):
                            g = t * 4 + s
                            norm_transpose(h[:, g, :], r_h[g], rstd[:, 2, g:g + 1], r_rstd[2][g], gb, r_gb, hs_bufs,
                                           hs_res, g, hnT3[:, :, g * 128:(g + 1) * 128], r_hnT3[g], scale=scale)
                            yield

                    steps = [(gi, t) for gi in range(len(FFN_GROUPS)) for t in range(NT3)]

                    def GU3(i):
                        gi, t = steps[i]
                        c0, G = FFN_GROUPS[gi]
                        bi = gi % 2
                        hm, rhm = hmid[i % 2], r_hmid[i % 2]
                        rds = [r_hnT3[t * 4 + s] for s in range(4)]
                        for c in range(G):
                            bg = PS.get()
                            mm_group(psf(bg), PS.res[bg],
                                     [(wg[bi][:, k, c * 128:(c + 1) * 128], hnT3[:, k, t * 512:(t + 1) * 512])
                                      for k in range(8)], [r_wg[bi]] + rds)
                            bu = PS.get()
                            mm_group(psf(bu), PS.res[bu],
                                     [(wu[bi][:, k, c * 128:(c + 1) * 128], hnT3[:, k, t * 512:(t + 1) * 512])
                                      for k in range(8)], [r_wu[bi]] + rds)
                            sg, rsg = sgb[c % 2], r_sg[c % 2]
                            S.op("act", lambda e: e.activation(out=sg[:], in_=psf(bg), func=AF.Silu),
                                 reads=[PS.res[bg]], writes=[rsg])
                            PS.free(bg)
                            S.op("dve", lambda e: e.tensor_tensor(out=hm[:, c, :], in0=psf(bu), in1=sg[:],
                                                                  op=ALU.mult), reads=[PS.res[bu], rsg], writes=[rhm])
                            PS.free(bu)
                            yield

                    def DN3(i):
                        gi, t = steps[i]
                        c0, G = FFN_GROUPS[gi]
                        bi = gi % 2
                        hm, rhm = hmid[i % 2], r_hmid[i % 2]
                        for s in range(4):
                            g = t * 4 + s
                            for hf in range(2):
                                b = PS.get()
                                mm_group(psf(b), PS.res[b],
                                         [(hm[:, c, s * 128:(s + 1) * 128], wd[bi][:, c, hf * 512:(hf + 1) * 512])
                                          for c in range(G)], [rhm, r_wd[bi]])
                                S.op("dve", lambda e: e.tensor_tensor(out=h[:, g, hf * 512:(hf + 1) * 512],
                                                                      in0=psf(b),
                                                                      in1=h[:, g, hf * 512:(hf + 1) * 512],
                                                                      op=ALU.add),
                                     reads=[PS.res[b], r_h[g]], writes=[r_h[g]])
                                PS.free(b)
                                yield
                            if gi == len(FFN_GROUPS) - 1:
                                sq_stats(junk, r_junk, 3, g)
                                finish_stats(3, g, g + 1)
                                y, ry = yb[g % 2], r_yb[g % 2]
                                if ry.w is None and not ry.r:
                                    for hr in (hs_res[2 * (g % 2)], hs_res[2 * (g % 2) + 1]):
                                        ry.w = hr.w
                                        for k_, v_ in hr.r.items():
                                            ry.r[k_] = max(ry.r.get(k_, 0), v_)
                                S.op("dve", lambda e: e.scalar_tensor_tensor(out=y, in0=h[:, g, :],
                                                                              scalar=rstd[:, 3, g:g + 1], in1=gbf[:],
                                                                              op0=ALU.mult, op1=ALU.mult),
                                     reads=[r_h[g], r_rstd[3][g], r_gbf], writes=[ry])
                                S.dma("sp", out_d[g * 128:(g + 1) * 128, :], y, reads=[ry])

                    A3s(0)
                    run(A3(0, scale=False))
                    A3s(1)
                    interleave((GU3(0), 1), (A3(1, scale=False), 1))
                    for i in range(len(steps)):
                        gi, t = steps[i]
                        nxt = GU3(i + 1) if i + 1 < len(steps) else None
                        extra = None
                        if gi == 0 and t + 2 < NT3:
                            A3s(t + 2)
                            extra = A3(t + 2, scale=False)
                        interleave((DN3(i), 1), (nxt, 1), (extra, 1))
                        if t == NT3 - 1 and gi + 2 < len(FFN_GROUPS):
                            load_group(gi + 2)
                    S.barrier()
    return nc


def make_in_maps(inputs):
    f = lambda a: np.ascontiguousarray(np.asarray(a, dtype=np.float32))
    x = f(inputs["x"])
    mem = f(inputs["mem"])
    B, SEQ, _ = x.shape
    per_b = SEQ // TOK
    assert B * per_b == NCORES
    b_in = f(inputs["b_in"])

    def pp(v):
        return np.ascontiguousarray(v.reshape(-1, 128).T)

    vec_base = np.concatenate([pp(b_in), pp(f(inputs["conv_b"])), pp(f(inputs["conv_ln_g"])),
                               pp(f(inputs["conv_ln_b"])), pp(f(inputs["gm_ln_g"])), pp(f(inputs["gm_ln_b"]))],
                              axis=1)
    conv_w = f(inputs["conv_w"])
    cw_pad = np.zeros((32, 512), np.float32)
    cw_pad[:KCONV] = conv_w
    cwp = np.ascontiguousarray(cw_pad.reshape(8, 4, 16, 32).transpose(1, 3, 2, 0).reshape(128, 16, 8))
    e32 = np.ascontiguousarray(np.tile(np.eye(32, dtype=np.float32), (4, 1)))
    w_s = f(inputs["gm_w_s"])
    wst = np.ascontiguousarray(w_s.transpose(2, 0, 1))
    sidx = np.arange(128)
    maskT = np.ascontiguousarray(
        np.broadcast_to((sidx[:, None] <= sidx[None, :]).astype(np.float32)[:, None, :], (128, 8, 128)))
    b_s = f(inputs["gm_b_s"])
    bs_rows = np.repeat(b_s, 64, axis=0).reshape(4, 128, 128).transpose(1, 0, 2)
    bs_rep = np.ascontiguousarray(bs_rows)
    ident = np.eye(128, dtype=np.float32)
    shared = {k: f(inputs[k]) for k in ("w_in", "w_out", "xa_wq", "xa_wkv", "xa_wo", "ffn_w_gate_up", "ffn_w_down",
                                        "norm_mix_g", "norm_xa_g", "norm_ffn_g", "final_norm_g", "mem_norm_g")}
    shared.update(bv_nat=np.ascontiguousarray(b_in[1536:2048]), gm_g_nat=f(inputs["gm_ln_g"]),
                  gm_b_nat=f(inputs["gm_ln_b"]))
    shared.update(cwp=cwp, E32=e32, wst=wst, maskT=maskT, bs_rep=bs_rep, ident=ident)
    in_maps = []
    for core in range(NCORES):
        b, part = divmod(core, per_b)
        t0 = part * TOK
        xs = x[b, t0:t0 + TOK]
        if t0 == 0:
            xh = np.zeros((128, D), np.float32)
            flag = 0.0
        else:
            xh = x[b, t0 - 128:t0]
            flag = 1.0
        vec = np.zeros((128, 40), np.float32)
        vec[:, :36] = vec_base
        vec[:, 36] = flag
        m = dict(shared)
        m.update(x=np.ascontiguousarray(xs), xh=np.ascontiguousarray(xh), mem=mem[b], vec=vec)
        in_maps.append(m)
    return in_maps, (B, SEQ, per_b)


def kernel(**inputs):
    in_maps, (B, SEQ, per_b) = make_in_maps(inputs)
    nc = build_nc()
    res = run_bass_kernel_spmd(nc, in_maps, core_ids=list(range(NCORES)))
    out = np.empty((B, SEQ, D), np.float32)
    for core in range(NCORES):
        b, part = divmod(core, per_b)
        out[b, part * TOK:(part + 1) * TOK] = res.results[core]["out"]
    return out
```

```python
import numpy as np
from contextlib import ExitStack
import concourse.bass as bass
import concourse.mybir as mybir
from concourse.bass_utils import run_bass_kernel_spmd

F32 = mybir.dt.float32
BF16 = mybir.dt.bfloat16
AF = mybir.ActivationFunctionType
ALU = mybir.AluOpType

NCORES = 8
D = 1024
TOK = 2048
NSUB = TOK // 128
MEM = 256
HID = 2816
NHC = HID // 128
FFN_GROUPS = [(0, 6), (6, 6), (12, 5), (17, 5)]
GMAX = 6
TS1 = 256
NT1 = TOK // TS1
KCONV = 31
RMS_EPS = 1e-6
LN_EPS = 1e-5


class Res:
    __slots__ = ("name", "w", "r")

    def __init__(self, name):
        self.name = name
        self.w = None
        self.r = {}


class Sched:
    NDS = 16

    def __init__(self, nc, es):
        self.nc = nc
        self.eng = dict(pe=nc.tensor, act=nc.scalar, dve=nc.vector, pool=nc.gpsimd, sp=nc.sync)
        self.sems = {}
        for k in self.eng:
            self.sems[k] = es.enter_context(nc.semaphore("s_" + k))
        self.cnt = {k: 0 for k in self.eng}
        self.seen = {k: {} for k in self.eng}
        self.logging = False
        self.log = {k: [] for k in self.eng}
        self.dq = {}
        for q in ("sp", "pool"):
            lst = []
            for i in range(self.NDS):
                key = "d_%s%d" % (q, i)
                self.sems[key] = es.enter_context(nc.semaphore(key))
                self.cnt[key] = 0
                lst.append(key)
            self.dq[q] = [lst, 0]

    def _where(self):
        import sys
        f = sys._getframe(2)
        names = []
        while f is not None and len(names) < 6:
            n = f.f_code.co_name
            if n == "build_nc":
                break
            if n not in ("<lambda>", "op", "mm_group", "run", "interleave", "step", "chain"):
                loc = f.f_locals
                tag = n
                for v in ("t", "i", "c", "s", "hd", "dc"):
                    if v in loc and isinstance(loc[v], int):
                        tag += " %s=%d" % (v, loc[v])
                names.append(tag)
            f = f.f_back
        return " < ".join(names)

    def _wait(self, e, deps):
        for key, val in sorted(deps):
            if key == e and e == "pe":
                continue
            if self.seen[e].get(key, 0) >= val:
                continue
            self.eng[e].wait_ge(self.sems[key], val)
            self.seen[e][key] = val

    def _deps(self, reads, writes):
        deps = set()
        for r in reads:
            if r.w is not None:
                deps.add(r.w)
        for w in writes:
            if w.w is not None:
                deps.add(w.w)
            for k, v in w.r.items():
                deps.add((k, v))
        return deps

    def _mark(self, mark, reads, writes):
        k, v = mark
        for r in reads:
            if r.r.get(k, 0) < v:
                r.r[k] = v
        for w in writes:
            w.w = mark
            w.r = {}

    def op(self, e, fn, reads=(), writes=(), signal=True):
        self._wait(e, self._deps(reads, writes))
        inst = fn(self.eng[e])
        if self.logging:
            self.log[e].append(self._where())
        if signal:
            self.cnt[e] += 1
            inst.then_inc(self.sems[e], 1)
            mark = (e, self.cnt[e])
        else:
            assert e == "pe"
            mark = (e, self.cnt[e] + 1)
        self._mark(mark, reads, writes)
        return inst

    def dma(self, q, out, in_, reads=(), writes=(), **kw):
        lst, nxt = self.dq[q]
        key = lst[nxt % self.NDS]
        self.dq[q][1] = nxt + 1
        deps = self._deps(reads, writes)
        if self.cnt[key] > 0:
            deps.add((key, self.cnt[key]))
        self._wait(q, deps)
        inst = self.eng[q].dma_start(out=out, in_=in_, **kw)
        self.cnt[key] += 16
        inst.then_inc(self.sems[key], 16)
        self._mark((key, self.cnt[key]), reads, writes)
        return inst

    def wait_all(self, e):
        deps = set()
        for k, v in self.cnt.items():
            if v > 0:
                deps.add((k, v))
        self._wait(e, deps)

    def barrier(self, dma=True):
        for e in self.eng:
            if dma:
                self.wait_all(e)
            else:
                self._wait(e, {(k, self.cnt[k]) for k in self.eng if self.cnt[k] > 0})


class PsPool:
    def __init__(self, nc, es, n=8):
        self.n = n
        self.t = [es.enter_context(nc.psum_tensor("psb%d" % i, [128, 512], F32)) for i in range(n)]
        self.res = [Res("psb%d" % i) for i in range(n)]
        self.held = [False] * n
        self.freeq = list(range(n))

    def get(self):
        if not self.freeq:
            raise RuntimeError("PSUM banks exhausted")
        i = self.freeq.pop(0)
        self.held[i] = True
        return i

    def free(self, i):
        assert self.held[i]
        self.held[i] = False
        self.freeq.append(i)


LOGGING = False


def run(gen):
    for _ in gen:
        pass


def interleave(*items):
    active = [[g, n] for g, n in items if g is not None]
    while active:
        for it in list(active):
            for _ in range(it[1]):
                try:
                    next(it[0])
                except StopIteration:
                    active.remove(it)
                    break


def step(gen, n):
    if gen is None:
        return
    for _ in range(n):
        try:
            next(gen)
        except StopIteration:
            return


def chain(*gens):
    for g in gens:
        for _ in g:
            yield


def build_nc():
    nc = bass.Bass("TRN2", target_bir_lowering=False)

    def din(name, shape):
        return nc.dram_tensor(name, list(shape), F32, kind="ExternalInput").ap()

    x_d = din("x", [TOK, D])
    xh_d = din("xh", [128, D])
    mem_d = din("mem", [MEM, D])
    w_in_d = din("w_in", [D, 2048])
    w_out_d = din("w_out", [D, D])
    wq_d = din("xa_wq", [D, D])
    wkv_d = din("xa_wkv", [D, 2 * D])
    wo_d = din("xa_wo", [D, D])
    wgu_d = din("ffn_w_gate_up", [D, 2 * HID])
    wd_d = din("ffn_w_down", [HID, D])
    g_mix_d = din("norm_mix_g", [D])
    g_xa_d = din("norm_xa_g", [D])
    g_ffn_d = din("norm_ffn_g", [D])
    g_fin_d = din("final_norm_g", [D])
    g_mem_d = din("mem_norm_g", [D])
    vec_d = din("vec", [128, 40])
    cwp_d = din("cwp", [128, 16, 8])
    e32_d = din("E32", [128, 32])
    wst_d = din("wst", [128, 8, 128])
    mask_d = din("maskT", [128, 8, 128])
    bs_d = din("bs_rep", [128, 4, 128])
    ident_d = din("ident", [128, 128])
    bv_d = din("bv_nat", [512])
    gg_d = din("gm_g_nat", [512])
    gbn_d = din("gm_b_nat", [512])
    out_d = nc.dram_tensor("out", [TOK, D], F32, kind="ExternalOutput").ap()

    es = ExitStack()
    with es:
        S = Sched(nc, es)
        S.logging = LOGGING
        nc._sched = S
        PS = PsPool(nc, es)

        def T(scope, name, shape, dt):
            return scope.enter_context(nc.sbuf_tensor("sb_" + name, list(shape), dt))

        def psf(b, n=512):
            return PS.t[b][:, 0:n]

        def psh(b, half, n):
            return PS.t[b][:, half * 256:half * 256 + n]

        def psb16(b):
            return PS.t[b][:].bitcast(BF16)

        def mm_group(out_ap, out_res, pairs, reads):
            n = len(pairs)
            for i, (l, r) in enumerate(pairs):
                S.op("pe", lambda e: e.matmul(out_ap, lhsT=l, rhs=r, start=(i == 0), stop=(i == n - 1)),
                     reads=reads, writes=[out_res], signal=(i == n - 1))

        h = T(es, "h", [128, NSUB, D], F32)
        r_h = [Res("h%d" % g) for g in range(NSUB)]
        identf = T(es, "identf", [128, 128], F32)
        identb = T(es, "identb", [128, 128], BF16)
        ones_b = T(es, "ones_b", [128, 128], BF16)
        vec = T(es, "vec", [128, 40], F32)
        stat = T(es, "stat", [128, 4, NSUB], F32)
        rstd = T(es, "rstd", [128, 4, NSUB], F32)
        stath = T(es, "stath", [128, 4], F32)
        r_ident, r_vec, r_ones = Res("ident"), Res("vec"), Res("ones")
        r_stat = [[Res("stat%d_%d" % (i, g)) for g in range(NSUB)] for i in range(4)]
        r_rstd = [[Res("rstd%d_%d" % (i, g)) for g in range(NSUB)] for i in range(4)]
        r_stath = Res("stath")

        S.dma("sp", identf[:], ident_d, writes=[r_ident])
        S.dma("sp", vec[:], vec_d, writes=[r_vec])
        S.op("dve", lambda e: e.tensor_copy(out=identb[:], in_=identf[:]), reads=[r_ident], writes=[r_ident])
        S.op("pool", lambda e: e.memset(ones_b[:], 1.0), writes=[r_ones])

        def bias(i):
            return vec[:, i:i + 1]

        def nt_scale(src_ap, src_res, rstd_ap, rstd_res, gb, r_gb, hs_bufs, hs_res, idx):
            hs = hs_bufs[idx % len(hs_bufs)]
            rh = hs_res[idx % len(hs_bufs)]
            S.op("dve", lambda e: e.scalar_tensor_tensor(out=hs[:], in0=src_ap, scalar=rstd_ap, in1=gb[:],
                                                          op0=ALU.mult, op1=ALU.mult),
                 reads=[src_res, rstd_res, r_gb], writes=[rh])

        def norm_transpose(src_ap, src_res, rstd_ap, rstd_res, gb, r_gb, hs_bufs, hs_res, idx, dst_ap, dst_res,
                           scale=True, evac="act"):
            hs = hs_bufs[idx % len(hs_bufs)]
            rh = hs_res[idx % len(hs_bufs)]
            if scale:
                nt_scale(src_ap, src_res, rstd_ap, rstd_res, gb, r_gb, hs_bufs, hs_res, idx)
            b = PS.get()
            pv = psb16(b)
            for c in range(8):
                S.op("pe", lambda e: e.transpose(out=pv[:, c * 128:(c + 1) * 128], in_=hs[:, c * 128:(c + 1) * 128],
                                                 identity=identb[:]),
                     reads=[rh, r_ident], writes=[PS.res[b]], signal=(c == 7))
            if evac == "act":
                S.op("act", lambda e: e.activation(out=dst_ap, in_=pv.rearrange("p (c t) -> p c t", c=8),
                                                   func=AF.Copy), reads=[PS.res[b]], writes=[dst_res])
            else:
                S.op("dve", lambda e: e.tensor_copy(out=dst_ap.bitcast(F32),
                                                    in_=psf(b).rearrange("p (c t) -> p c t", c=8)),
                     reads=[PS.res[b]], writes=[dst_res])
            PS.free(b)

        def finish_stats(si, lo, hi):
            rs = [r_stat[si][g] for g in range(lo, hi)]
            ws = [r_rstd[si][g] for g in range(lo, hi)]
            S.op("act", lambda e: e.activation(out=rstd[:, si, lo:hi], in_=stat[:, si, lo:hi], func=AF.Ln,
                                               scale=1.0 / D, bias=RMS_EPS), reads=rs, writes=ws)
            S.op("act", lambda e: e.activation(out=rstd[:, si, lo:hi], in_=rstd[:, si, lo:hi], func=AF.Exp,
                                               scale=-0.5), reads=ws, writes=ws)

        def sq_stats(junk, r_junk, si, g, eng="act"):
            if eng == "act":
                S.op("act", lambda e: e.activation(out=junk[:], in_=h[:, g, :], func=AF.Square,
                                                   accum_out=stat[:, si, g:g + 1]),
                     reads=[r_h[g]], writes=[r_junk, r_stat[si][g]])
            else:
                S.op("dve", lambda e: e.scalar_tensor_tensor(out=junk[:], in0=h[:, g, :], scalar=1.0, in1=h[:, g, :],
                                                              op0=ALU.mult, op1=ALU.mult,
                                                              accum_out=stat[:, si, g:g + 1]),
                     reads=[r_h[g]], writes=[r_junk, r_stat[si][g]])

        s1 = ExitStack()
        with s1:
            w_in = T(s1, "w_in", [128, 8, 2048], BF16)
            w_out = T(s1, "w_out", [128, 8, D], BF16)
            r_wout = Res("w_out")
            r_winb = [Res("w_in%d" % i) for i in range(4)]
            w_in_v = w_in_d.rearrange("(k p) n -> p k n", p=128)
            for blk in (1, 0, 2, 3):
                S.dma("pool", w_in[:, :, blk * 512:(blk + 1) * 512], w_in_v[:, :, blk * 512:(blk + 1) * 512],
                      writes=[r_winb[blk]])
            win_mark = r_winb[3].w
            gb = T(s1, "gb1", [128, D], F32)
            r_gb = Res("gb1")
            cwp = T(s1, "cwp", [128, 16, 8], F32)
            e32 = T(s1, "e32", [128, 32], F32)
            r_cw = Res("cwp")
            r_e32 = Res("e32")
            wsT = T(s1, "wsT", [128, 8, 128], BF16)
            r_wsT = Res("wsT")
            bs = T(s1, "bs", [128, 4, 128], F32)
            r_bs = Res("bs")
            S.dma("sp", gb[:], g_mix_d.partition_broadcast(128), writes=[r_gb])
            junk = T(s1, "junk1", [128, D], BF16)
            r_junk = Res("junk1")
            hs_bufs = [T(s1, "hs1_%d" % i, [128, D], BF16) for i in range(2)]
            hs_res = [Res("hs1_%d" % i) for i in range(2)]
            hnTh = T(s1, "hnTh", [128, 8, 128], BF16)
            r_hnTh = Res("hnTh")
            for g in range(2):
                S.dma("sp", h[:, g, :], x_d[g * 128:(g + 1) * 128, :], writes=[r_h[g]])
            s1s = ExitStack()
            with s1s:
                xhalo = T(s1s, "xhalo", [128, D], F32)
                r_xhalo = Res("xhalo")
                S.dma("sp", xhalo[:], xh_d, writes=[r_xhalo])
                S.dma("sp", cwp[:], cwp_d, writes=[r_cw])
                S.dma("sp", e32[:], e32_d, writes=[r_e32])
                S.dma("sp", bs[:], bs_d, writes=[r_bs])
                wst_f = T(s1s, "wst_f", [128, 8, 128], F32)
                mask_f = T(s1s, "mask_f", [128, 8, 128], F32)
                r_wstf, r_maskf = Res("wstf"), Res("maskf")
                S.dma("sp", wst_f[:], wst_d, writes=[r_wstf])
                S.dma("sp", mask_f[:], mask_d, writes=[r_maskf])
                S.op("dve", lambda e: e.tensor_tensor(out=wsT[:], in0=wst_f[:], in1=mask_f[:], op=ALU.mult),
                     reads=[r_wstf, r_maskf], writes=[r_wsT])
                S.op("act", lambda e: e.activation(out=junk[:], in_=xhalo[:], func=AF.Square,
                                                   accum_out=stath[:, 0:1]),
                     reads=[r_xhalo], writes=[r_junk, r_stath])
                S.op("act", lambda e: e.activation(out=stath[:, 1:2], in_=stath[:, 0:1], func=AF.Ln, scale=1.0 / D,
                                                   bias=RMS_EPS), reads=[r_stath], writes=[r_stath])
                S.op("act", lambda e: e.activation(out=stath[:, 1:2], in_=stath[:, 1:2], func=AF.Exp, scale=-0.5),
                     reads=[r_stath], writes=[r_stath])
                norm_transpose(xhalo[:], r_xhalo, stath[:, 1:2], r_stath, gb, r_gb, hs_bufs, hs_res, 0, hnTh[:],
                               r_hnTh)
                for e_ in S.eng:
                    S._wait(e_, {("dve", S.cnt["dve"]), ("act", S.cnt["act"])})
            S._wait("sp", {r_winb[0].w})
            for g in range(2, 4):
                S.dma("sp", h[:, g, :], x_d[g * 128:(g + 1) * 128, :], writes=[r_h[g]])
            S.dma("pool", w_out[:], w_out_d.rearrange("(k p) n -> p k n", p=128), writes=[r_wout])

            WP = T(s1, "WP", [128, 16, 8, 32], BF16)
            r_WP = Res("WP")
            S.op("pool", lambda e: e.tensor_tensor(
                out=WP[:].rearrange("p q j c -> p (q j) c"),
                in0=e32[:].unsqueeze(1).broadcast_to([128, 128, 32]),
                in1=cwp[:].rearrange("p q j -> p (q j)").unsqueeze(2).broadcast_to([128, 128, 32]), op=ALU.mult),
                 reads=[r_cw, r_e32], writes=[r_WP])
            WX = 128 + TS1
            WC = WX - 2
            AXT = T(s1, "AXT", [128, 16, WX], BF16)
            r_AXT = [[Res("AXT%d_%d" % (qq, jj)) for jj in range(4)] for qq in range(4)]

            def relayout(axc, raxc):
                for qq in range(4):
                    for jj in range(4):
                        dst = AXT[jj * 32:(jj + 1) * 32, :, 0:WC].rearrange("p (c q) w -> p c q w", q=4)[:, :, qq, :]
                        S.dma("sp", dst, axc[qq * 32:(qq + 1) * 32, :, jj:jj + WC], reads=list(raxc),
                              writes=[r_AXT[qq][jj]])

            hnT = [T(s1, "hnT1_0", [128, 8, TS1], BF16)] * 2
            r_hnT = [Res("hnT1_0")] * 2
            ax = [T(s1, "ax%d" % i, [128, 4, WX + 2], BF16) for i in range(2)]
            r_ax = [[Res("ax%d_%d" % (i, c)) for c in range(4)] for i in range(2)]
            for i in range(2):
                S.op("pool", lambda e: e.memset(ax[i][:], 0.0), writes=r_ax[i])
            sig = [T(s1, "sig%d" % i, [128, TS1], BF16) for i in range(2)]
            r_sig = [Res("sig%d" % i) for i in range(2)]
            u_t = [T(s1, "u_t%d" % i, [128, 4, TS1], BF16) for i in range(2)]
            r_u = [Res("u%d" % i) for i in range(2)]
            bvt = T(s1, "bvt", [128, 512], F32)
            ggt = T(s1, "ggt", [128, 512], F32)
            gbt = T(s1, "gbt", [128, 512], F32)
            r_bvt, r_ggt, r_gbt = Res("bvt"), Res("ggt"), Res("gbt")
            S.dma("sp", bvt[:], bv_d.partition_broadcast(128), writes=[r_bvt])
            S.dma("sp", ggt[:], gg_d.partition_broadcast(128), writes=[r_ggt])
            S.dma("sp", gbt[:], gbn_d.partition_broadcast(128), writes=[r_gbt])
            zb = [T(s1, "zb%d" % i, [128, 512], F32) for i in range(2)]
            r_zb = [Res("zb%d" % i) for i in range(2)]
            vst = T(s1, "vst", [128, 8], F32)
            r_vst = Res("vst")
            By1 = T(s1, "By1", [128, 4, TS1], BF16)
            By2 = T(s1, "By2", [128, 4, TS1], BF16)
            r_By1 = [Res("By1_c%d" % c) for c in range(4)]
            r_By2 = [Res("By2_c%d" % c) for c in range(4)]
            vtm = [T(s1, "vtm%d" % i, [128, 512], BF16) for i in range(2)]
            r_vtm = [Res("vtm%d" % i) for i in range(2)]
            mixT = T(s1, "mixT", [128, 8, TS1], BF16)
            r_mixc, r_mixg = Res("mixT_conv"), Res("mixT_gm")
            lnb = {}
            for nm in ("E",):
                lnb[nm] = (T(s1, "ln_mean" + nm, [128, TS1], F32), T(s1, "ln_rstd" + nm, [128, TS1], F32),
                           Res("ln_mean" + nm), Res("ln_rstd" + nm))
            t1 = [T(s1, "t1_%d" % i, [128, TS1], F32) for i in range(2)]
            r_t1 = [Res("t1_%d" % i) for i in range(2)]
            gmt_all = T(s1, "gmt", [128, 2 * TS1], F32)
            gmt = gmt_all[:, 0:TS1]
            gmt_junk = gmt_all[:].bitcast(BF16)
            r_gmt = Res("gmt")
            nsub = TS1 // 128

            def stats1(t):
                for s in range(nsub):
                    sq_stats(junk, r_junk, 0, t * nsub + s)
                finish_stats(0, t * nsub, (t + 1) * nsub)

            stats1(0)
            def glu_chunk(c, hn_ap, r_hn, n, dst_list, maskit=False):
                sg = sig[c % 2]
                rs = r_sig[c % 2]
                b = PS.get()
                mm_group(psf(b, n), PS.res[b],
                         [(w_in[:, k, (4 + c) * 128:(5 + c) * 128], hn_ap(k)) for k in range(8)], [r_winb[1], r_hn])
                S.op("act", lambda e: e.activation(out=sg[:, 0:n], in_=psf(b, n), func=AF.Sigmoid, bias=bias(4 + c)),
                     reads=[PS.res[b], r_vec], writes=[rs])
                PS.free(b)
                b = PS.get()
                mm_group(psf(b, n), PS.res[b],
                         [(w_in[:, k, c * 128:(c + 1) * 128], hn_ap(k)) for k in range(8)], [r_winb[0], r_hn])
                for (dst, rd, lo, hi) in dst_list:
                    S.op("dve", lambda e: e.scalar_tensor_tensor(out=dst, in0=PS.t[b][:, lo:hi], scalar=bias(c),
                                                                  in1=sg[:, lo:hi], op0=ALU.add, op1=ALU.mult),
                         reads=[PS.res[b], rs, r_vec], writes=[rd])
                    if maskit:
                        S.op("dve", lambda e: e.tensor_scalar(out=dst, in0=dst, scalar1=vec[:, 36:37], scalar2=None,
                                                              op0=ALU.mult), reads=[rd, r_vec], writes=[rd])
                PS.free(b)

            def A1s(t):
                for s in range(nsub):
                    g = t * nsub + s
                    nt_scale(h[:, g, :], r_h[g], rstd[:, 0, g:g + 1], r_rstd[0][g], gb, r_gb, hs_bufs, hs_res, g)

            def A1(t, scale=True):
                hb, rhb = hnT[t % 2], r_hnT[t % 2]
                for s in range(nsub):
                    g = t * nsub + s
                    norm_transpose(h[:, g, :], r_h[g], rstd[:, 0, g:g + 1], r_rstd[0][g], gb, r_gb, hs_bufs, hs_res,
                                   g, hb[:, :, s * 128:(s + 1) * 128], rhb, scale=scale, evac="dve")
                    yield

            def B1(t):
                hb, rhb = hnT[t % 2], r_hnT[t % 2]
                axc, raxc = ax[t % 2], r_ax[t % 2]
                axn, raxn = ax[(t + 1) % 2], r_ax[(t + 1) % 2]
                hn_ap = lambda k: hb[:, k, :]
                for c in range(4):
                    dsts = [(axc[:, c, 128:128 + TS1], raxc[c], 0, TS1)]
                    if t + 1 < NT1:
                        dsts.append((axn[:, c, 0:128], raxn[c], TS1 - 128, TS1))
                    glu_chunk(c, hn_ap, rhb, TS1, dsts)
                    if c == 3:
                        relayout(axc, raxc)
                    yield
                for c in range(4):
                    b = PS.get()
                    mm_group(psf(b, TS1), PS.res[b],
                             [(w_in[:, k, (8 + c) * 128:(9 + c) * 128], hn_ap(k)) for k in range(8)],
                             [r_winb[2], rhb])
                    S.op("act", lambda e: e.activation(out=u_t[t % 2][:, c, :], in_=psf(b, TS1),
                                                       func=AF.Gelu_apprx_tanh, bias=bias(8 + c)),
                         reads=[PS.res[b], r_vec], writes=[r_u[t % 2]])
                    PS.free(b)
                    yield
                for s in range(nsub):
                    b = PS.get()
                    mm_group(psf(b), PS.res[b],
                             [(hb[:, k, s * 128:(s + 1) * 128], w_in[:, k, 1536:2048]) for k in range(8)],
                             [r_winb[3], rhb])
                    S.op("dve", lambda e: e.tensor_tensor(out=zb[s][:], in0=psf(b), in1=bvt[:], op=ALU.add),
                         reads=[PS.res[b], r_bvt], writes=[r_zb[s]])
                    PS.free(b)
                    S.op("act", lambda e: e.activation(out=zb[s][:], in_=zb[s][:], func=AF.Gelu_apprx_tanh,
                                                       accum_out=vst[:, s:s + 1]),
                         reads=[r_zb[s]], writes=[r_zb[s], r_vst])
                    S.op("act", lambda e: e.activation(out=junk[:, 0:512], in_=zb[s][:], func=AF.Square,
                                                       accum_out=vst[:, 2 + s:3 + s]),
                         reads=[r_zb[s]], writes=[r_junk, r_vst])
                    yield
                S.op("dve", lambda e: e.tensor_scalar(out=vst[:, 4:6], in0=vst[:, 0:2], scalar1=1.0 / 512, scalar2=None,
                                                      op0=ALU.mult), reads=[r_vst], writes=[r_vst])
                S.op("dve", lambda e: e.tensor_tensor(out=vst[:, 6:8], in0=vst[:, 4:6], in1=vst[:, 4:6], op=ALU.mult),
                     reads=[r_vst], writes=[r_vst])
                S.op("dve", lambda e: e.scalar_tensor_tensor(out=vst[:, 6:8], in0=vst[:, 2:4], scalar=1.0 / 512,
                                                              in1=vst[:, 6:8], op0=ALU.mult, op1=ALU.subtract),
                     reads=[r_vst], writes=[r_vst])
                S.op("act", lambda e: e.activation(out=vst[:, 6:8], in_=vst[:, 6:8], func=AF.Ln, bias=LN_EPS),
                     reads=[r_vst], writes=[r_vst])
                S.op("act", lambda e: e.activation(out=vst[:, 6:8], in_=vst[:, 6:8], func=AF.Exp, scale=-0.5),
                     reads=[r_vst], writes=[r_vst])
                for s in range(nsub):
                    S.op("dve", lambda e: e.scalar_tensor_tensor(out=zb[s][:], in0=zb[s][:], scalar=vst[:, 4 + s:5 + s],
                                                                  in1=ggt[:], op0=ALU.subtract, op1=ALU.mult),
                         reads=[r_zb[s], r_vst, r_ggt], writes=[r_zb[s]])
                    S.op("dve", lambda e: e.scalar_tensor_tensor(out=vtm[s][:], in0=zb[s][:], scalar=vst[:, 6 + s:7 + s],
                                                                  in1=gbt[:], op0=ALU.mult, op1=ALU.add),
                         reads=[r_zb[s], r_vst, r_gbt], writes=[r_vtm[s]])
                yield

            def ln_stats(nm, src, r_src, srcsq, r_sq):
                mean, rs_, r_m, r_r = lnb[nm]
                b1 = PS.get()
                mm_group(psf(b1, TS1), PS.res[b1], [(ones_b[:], src[:, c, :]) for c in range(4)], [r_ones] + list(r_src))
                b2 = PS.get()
                mm_group(psf(b2, TS1), PS.res[b2], [(ones_b[:], srcsq[:, c, :]) for c in range(4)], [r_ones] + list(r_sq))
                S.op("act", lambda e: e.activation(out=mean[:], in_=psf(b1, TS1), func=AF.Copy, scale=1.0 / 512),
                     reads=[PS.res[b1]], writes=[r_m])
                S.op("act", lambda e: e.activation(out=rs_[:], in_=psf(b1, TS1), func=AF.Square, scale=1.0 / 512),
                     reads=[PS.res[b1]], writes=[r_r])
                PS.free(b1)
                S.op("dve", lambda e: e.scalar_tensor_tensor(out=rs_[:], in0=psf(b2, TS1), scalar=1.0 / 512,
                                                              in1=rs_[:], op0=ALU.mult, op1=ALU.subtract),
                     reads=[PS.res[b2], r_r], writes=[r_r])
                PS.free(b2)
                S.op("act", lambda e: e.activation(out=rs_[:], in_=rs_[:], func=AF.Ln, bias=LN_EPS),
                     reads=[r_r], writes=[r_r])
                S.op("act", lambda e: e.activation(out=rs_[:], in_=rs_[:], func=AF.Exp, scale=-0.5),
                     reads=[r_r], writes=[r_r])

            def ln_apply(nm, src, r_src, func, gi, bi, dst_fn, r_dst):
                mean, rs_, r_m, r_r = lnb[nm]
                for c in range(4):
                    tt = t1[c % 2]
                    rt = r_t1[c % 2]
                    S.op("dve", lambda e: e.tensor_tensor(out=tt[:], in0=src[:, c, :], in1=mean[:],
                                                          op=ALU.subtract), reads=[r_src[c], r_m], writes=[rt])
                    S.op("dve", lambda e: e.tensor_tensor(out=tt[:], in0=tt[:], in1=rs_[:], op=ALU.mult),
                         reads=[rt, r_r], writes=[rt])
                    S.op("act", lambda e: e.activation(out=dst_fn(c), in_=tt[:], func=func, scale=bias(gi + c),
                                                       bias=bias(bi + c)), reads=[rt, r_vec], writes=[r_dst])
                    yield

            def C1(t):
                axc, raxc = ax[t % 2], r_ax[t % 2]
                for c in range(4):
                    b = PS.get()
                    for J in range(8):
                        for qq in range(4):
                            q = 4 * c + qq
                            S.op("pe", lambda e: e.matmul(PS.t[b][32 * qq:32 * qq + 32, 0:TS1], lhsT=WP[:, q, J, :],
                                                          rhs=AXT[:, q, 98 + 4 * J:98 + 4 * J + TS1],
                                                          start=(J == 0), stop=(J == 7), tile_position=(0, 32 * qq)),
                                 reads=[r_WP] + r_AXT[qq], writes=[PS.res[b]], signal=(J == 7 and qq == 3))
                    S.op("act", lambda e: e.activation(out=By1[:, c, :], in_=psf(b, TS1), func=AF.Identity,
                                                       bias=bias(16 + c)), reads=[PS.res[b], r_vec], writes=[r_By1[c]])
                    PS.free(b)
                    S.op("pool", lambda e: e.tensor_tensor(out=By2[:, c, :], in0=By1[:, c, :], in1=By1[:, c, :],
                                                          op=ALU.mult), reads=[r_By1[c]], writes=[r_By2[c]])
                    yield

            def F1(t):
                u_c, r_uc = u_t[t % 2], r_u[t % 2]
                bm2 = [PS.get() for _ in range(2)]
                bm = [bm2[0], bm2[0], bm2[1], bm2[1]]
                for s in range(nsub):
                    vt, rvt = vtm[s % 2], r_vtm[s % 2]
                    for c in range(4):
                        for hh in range(2):
                            hd = 2 * c + hh
                            last = (s == nsub - 1) and (hh == 1)
                            S.op("pe", lambda e: e.matmul(PS.t[bm[c]][hh * 64:(hh + 1) * 64,
                                                                       (c % 2) * 256 + s * 128:(c % 2) * 256 + (s + 1) * 128],
                                                          lhsT=vt[:, hd * 64:(hd + 1) * 64], rhs=wsT[:, hd, :],
                                                          start=True, stop=True),
                                 reads=[rvt, r_wsT], writes=[PS.res[bm[c]]], signal=last)
                    yield
                for c in range(4):
                    S.op("dve", lambda e: e.tensor_tensor(
                        out=gmt[:].rearrange("p (s t) -> p s t", s=nsub),
                        in0=psh(bm[c], c % 2, TS1).rearrange("p (s t) -> p s t", s=nsub),
                        in1=bs[:, c, :].unsqueeze(1).broadcast_to([128, nsub, 128]), op=ALU.add),
                         reads=[PS.res[bm[c]], r_bs], writes=[r_gmt])
                    S.op("dve", lambda e: e.tensor_tensor(out=mixT[:, 4 + c, :], in0=gmt[:], in1=u_c[:, c, :],
                                                          op=ALU.mult), reads=[r_gmt, r_uc], writes=[r_mixg])
                    if c % 2 == 1:
                        PS.free(bm[c])
                    yield

            def G1(t):
                for s in range(nsub):
                    g = t * nsub + s
                    for hf in range(2):
                        b = PS.get()
                        mm_group(psf(b), PS.res[b],
                                 [(mixT[:, k, s * 128:(s + 1) * 128], w_out[:, k, hf * 512:(hf + 1) * 512])
                                  for k in range(8)], [r_mixc, r_mixg, r_wout])
                        S.op("dve", lambda e: e.tensor_tensor(out=h[:, g, hf * 512:(hf + 1) * 512], in0=psf(b),
                                                              in1=h[:, g, hf * 512:(hf + 1) * 512], op=ALU.add),
                             reads=[PS.res[b], r_h[g]], writes=[r_h[g]])
                        PS.free(b)
                        yield
                    sq_stats(gmt_junk, r_gmt, 1, g, eng="dve")

            A1s(0)
            run(A1(0, scale=False))
            stats1(1)
            for c in range(4):
                glu_chunk(c, lambda k: hnTh[:, k, :], r_hnTh, 128, [(ax[0][:, c, 0:128], r_ax[0][c], 0, 128)],
                          maskit=True)
            gB0 = B1(0)
            step(gB0, 8)
            S._wait("sp", {win_mark})
            for g in range(4, NSUB):
                S.dma("sp", h[:, g, :], x_d[g * 128:(g + 1) * 128, :], writes=[r_h[g]])
            A1s(1)
            run(gB0)
            run(A1(1, scale=False))
            stats1(2)
            for t in range(NT1):
                if t + 2 < NT1:
                    A1s(t + 2)
                run(C1(t))
                gB = B1(t + 1) if t + 1 < NT1 else None
                step(gB, 2)
                ln_stats("E", By1, r_By1, By2, r_By2)
                step(gB, 2)
                gF = F1(t)
                step(gF, 1)
                step(gB, 1)
                step(gF, 1)
                step(gB, 1)
                step(gF, 1)
                step(gB, 1)
                run(ln_apply("E", By1, r_By1, AF.Silu, 20, 24, lambda c: mixT[:, c, :], r_mixc))
                run(gF)
                if gB is not None:
                    run(gB)
                if t + 3 < NT1:
                    stats1(t + 3)
                if t + 2 < NT1:
                    run(A1(t + 2, scale=False))
                run(G1(t))
            S.barrier()

        s23 = ExitStack()
        with s23:
            wg = [None, None]
            wu = [None, None]
            wd = [None, None]
            wg[0] = T(s23, "wg0", [128, 8, GMAX * 128], BF16)
            wu[0] = T(s23, "wu0", [128, 8, GMAX * 128], BF16)
            wd[0] = T(s23, "wd0", [128, GMAX, D], BF16)
            gb3 = T(s23, "gb3", [128, D], F32)
            r_gb3 = Res("gb3")
            r_wg = [Res("wg%d" % i) for i in range(2)]
            r_wu = [Res("wu%d" % i) for i in range(2)]
            r_wd = [Res("wd%d" % i) for i in range(2)]

            def load_group(gi):
                c0, G = FFN_GROUPS[gi]
                bi = gi % 2
                S.dma("pool", wg[bi][:, :, 0:G * 128],
                      wgu_d[:, c0 * 128:(c0 + G) * 128].rearrange("(k p) n -> p k n", p=128), writes=[r_wg[bi]])
                S.dma("pool", wu[bi][:, :, 0:G * 128],
                      wgu_d[:, HID + c0 * 128:HID + (c0 + G) * 128].rearrange("(k p) n -> p k n", p=128),
                      writes=[r_wu[bi]])
                S.dma("pool", wd[bi][:, 0:G, :],
                      wd_d[c0 * 128:(c0 + G) * 128, :].rearrange("(g p) n -> p g n", p=128), writes=[r_wd[bi]])

            s2 = ExitStack()
            with s2:
                wq = T(s2, "wq", [128, 8, D], BF16)
                wo = T(s2, "wo", [128, 8, D], BF16)
                KT = T(s2, "KT", [128, 8, MEM], BF16)
                Vt = T(s2, "Vt", [128, 2, D], BF16)
                r_wq, r_wo, r_KT, r_V = Res("wq"), Res("wo"), Res("KT"), Res("V")
                gb = T(s2, "gb2", [128, D], F32)
                r_gb = Res("gb2")
                hs_bufs = [T(s2, "hs2_%d" % i, [128, D], BF16) for i in range(4)]
                hs_res = [Res("hs2_%d" % i) for i in range(4)]
                junk = T(s2, "junk2", [128, D], BF16)
                r_junk = Res("junk2")
                s2a = ExitStack()
                with s2a:
                    wkv = T(s2a, "wkv", [128, 8, 2 * D], BF16)
                    r_wkh = [Res("wk%d" % i) for i in range(2)]
                    r_wvh = [Res("wv%d" % i) for i in range(2)]
                    wkv_v = wkv_d.rearrange("(k p) n -> p k n", p=128)
                    for i in range(2):
                        S.dma("pool", wkv[:, :, i * 512:(i + 1) * 512], wkv_v[:, :, i * 512:(i + 1) * 512],
                              writes=[r_wkh[i]])
                    for i in range(2):
                        S.dma("pool", wkv[:, :, D + i * 512:D + (i + 1) * 512],
                              wkv_v[:, :, D + i * 512:D + (i + 1) * 512], writes=[r_wvh[i]])
                    S.dma("pool", wq[:], wq_d.rearrange("(k p) n -> p k n", p=128), writes=[r_wq])
                    S.dma("pool", wo[:], wo_d.rearrange("(k p) n -> p k n", p=128), writes=[r_wo])
                    load_group(0)
                    S.dma("sp", gb3[:], g_ffn_d.partition_broadcast(128), writes=[r_gb3])
                    msb = T(s2a, "msb", [128, 2, D], F32)
                    r_msb = Res("msb")
                    gbm = T(s2a, "gbm", [128, D], F32)
                    r_gbm = Res("gbm")
                    mnT = T(s2a, "mnT", [128, 8, MEM], BF16)
                    r_mnT = Res("mnT")
                    S.dma("sp", msb[:], mem_d.rearrange("(c p) n -> p c n", p=128), writes=[r_msb])
                    S.dma("sp", gbm[:], g_mem_d.partition_broadcast(128), writes=[r_gbm])
                    S.dma("sp", gb[:], g_xa_d.partition_broadcast(128), writes=[r_gb])
                    for mc in range(2):
                        S.op("act", lambda e: e.activation(out=junk[:], in_=msb[:, mc, :], func=AF.Square,
                                                           accum_out=stath[:, 2 + mc:3 + mc]),
                             reads=[r_msb], writes=[r_junk, r_stath])
                    S.op("act", lambda e: e.activation(out=stath[:, 2:4], in_=stath[:, 2:4], func=AF.Ln,
                                                       scale=1.0 / D, bias=RMS_EPS), reads=[r_stath], writes=[r_stath])
                    S.op("act", lambda e: e.activation(out=stath[:, 2:4], in_=stath[:, 2:4], func=AF.Exp,
                                                       scale=-0.5), reads=[r_stath], writes=[r_stath])
                    finish_stats(1, 0, NSUB)
                    for mc in range(2):
                        norm_transpose(msb[:, mc, :], r_msb, stath[:, 2 + mc:3 + mc], r_stath, gbm, r_gbm, hs_bufs,
                                       hs_res, mc, mnT[:, :, mc * 128:(mc + 1) * 128], r_mnT)
                    for dc in range(8):
                        b = PS.get()
                        mm_group(psf(b, MEM), PS.res[b],
                                 [(wkv[:, k, dc * 128:(dc + 1) * 128], mnT[:, k, :]) for k in range(8)],
                                 [r_wkh[dc // 4], r_mnT])
                        S.op("act", lambda e: e.activation(out=KT[:, dc, :], in_=psf(b, MEM), func=AF.Copy),
                             reads=[PS.res[b]], writes=[r_KT])
                        PS.free(b)
                    for mc in range(2):
                        for hf in range(2):
                            b = PS.get()
                            mm_group(psf(b), PS.res[b],
                                     [(mnT[:, k, mc * 128:(mc + 1) * 128],
                                       wkv[:, k, D + hf * 512:D + (hf + 1) * 512]) for k in range(8)], [r_wvh[hf], r_mnT])
                            S.op("act", lambda e: e.activation(out=Vt[:, mc, hf * 512:(hf + 1) * 512], in_=psf(b),
                                                               func=AF.Copy), reads=[PS.res[b]], writes=[r_V])
                            PS.free(b)
                    S.barrier(dma=False)
                hnT = [T(s2, "hnT2_0", [128, 8, 512], BF16)] * 2
                r_hnT = [Res("hnT2_0")] * 2
                qT = [T(s2, "qT%d" % i, [128, 8, 512], BF16) for i in range(2)]
                r_qT = [[Res("qT%d_%d" % (i, dc)) for dc in range(8)] for i in range(2)]
                PT = T(s2, "PT", [128, 4, 2, 512], BF16)
                rden = T(s2, "rden", [128, 4, 512], F32)
                oT = T(s2, "oT", [128, 8, 512], BF16)
                r_PT = [Res("PT%d" % i) for i in range(4)]
                r_rden = [Res("rden%d" % i) for i in range(4)]
                r_oT = Res("oT")
                SCALE = 256 ** -0.5
                NT2 = TOK // 512

                def A2s(t):
                    for s in range(4):
                        g = t * 4 + s
                        nt_scale(h[:, g, :], r_h[g], rstd[:, 1, g:g + 1], r_rstd[1][g], gb, r_gb, hs_bufs, hs_res, g)

                def A2(t, scale=True):
                    hb, rhb = hnT[t % 2], r_hnT[t % 2]
                    for s in range(4):
                        g = t * 4 + s
                        norm_transpose(h[:, g, :], r_h[g], rstd[:, 1, g:g + 1], r_rstd[1][g], gb, r_gb, hs_bufs,
                                       hs_res, g, hb[:, :, s * 128:(s + 1) * 128], rhb, scale=scale, evac="dve")
                        yield

                def Q2(t):
                    hb, rhb = hnT[t % 2], r_hnT[t % 2]
                    for dc in range(8):
                        b = PS.get()
                        mm_group(psf(b), PS.res[b],
                                 [(wq[:, k, dc * 128:(dc + 1) * 128], hb[:, k, :]) for k in range(8)], [r_wq, rhb])
                        if dc % 2 == 0:
                            S.op("act", lambda e: e.activation(out=qT[t % 2][:, dc, :], in_=psf(b), func=AF.Copy),
                                 reads=[PS.res[b]], writes=[r_qT[t % 2][dc]])
                        else:
                            S.op("dve", lambda e: e.tensor_copy(out=qT[t % 2][:, dc, :], in_=psf(b)),
                                 reads=[PS.res[b]], writes=[r_qT[t % 2][dc]])
                        PS.free(b)
                        yield

                def SH2(t):
                    q, rq = qT[t % 2], r_qT[t % 2]

                    def sc(hd):
                        for mc in range(2):
                            b = PS.get()
                            mm_group(psf(b), PS.res[b],
                                     [(KT[:, 2 * hd + i, mc * 128:(mc + 1) * 128], q[:, 2 * hd + i, :])
                                      for i in range(2)], [r_KT, rq[2 * hd], rq[2 * hd + 1]])
                            S.op("act", lambda e: e.activation(out=PT[:, hd, mc, :], in_=psf(b), func=AF.Exp,
                                                               scale=SCALE), reads=[PS.res[b]], writes=[r_PT[hd]])
                            PS.free(b)

                    def dn(hd):
                        b = PS.get()
                        mm_group(psf(b), PS.res[b], [(ones_b[:], PT[:, hd, mc, :]) for mc in range(2)],
                                 [r_ones, r_PT[hd]])
                        S.op("act", lambda e: e.activation(out=rden[:, hd, :], in_=psf(b), func=AF.Ln),
                             reads=[PS.res[b]], writes=[r_rden[hd]])
                        PS.free(b)
                        S.op("act", lambda e: e.activation(out=rden[:, hd, :], in_=rden[:, hd, :], func=AF.Exp,
                                                           scale=-1.0), reads=[r_rden[hd]], writes=[r_rden[hd]])

                    def pv_(hd):
                        for i in range(2):
                            dc = 2 * hd + i
                            b = PS.get()
                            mm_group(psf(b), PS.res[b],
                                     [(Vt[:, mc, dc * 128:(dc + 1) * 128], PT[:, hd, mc, :]) for mc in range(2)],
                                     [r_V, r_PT[hd]])
                            S.op("dve", lambda e: e.tensor_tensor(out=oT[:, dc, :], in0=psf(b), in1=rden[:, hd, :],
                                                                  op=ALU.mult),
                                 reads=[PS.res[b], r_rden[hd]], writes=[r_oT])
                            PS.free(b)

                    seq = [(sc, 0), (sc, 1), (sc, 2), (dn, 0), (sc, 3), (dn, 1), (pv_, 0), (dn, 2), (pv_, 1),
                           (dn, 3), (pv_, 2), (pv_, 3)]
                    for fn, hd in seq:
                        fn(hd)
                        yield

                def O2(t):
                    for s in range(4):
                        g = t * 4 + s
                        for hf in range(2):
                            b = PS.get()
                            mm_group(psf(b), PS.res[b],
                                     [(oT[:, k, s * 128:(s + 1) * 128], wo[:, k, hf * 512:(hf + 1) * 512])
                                      for k in range(8)], [r_oT, r_wo])
                            S.op("dve", lambda e: e.tensor_tensor(out=h[:, g, hf * 512:(hf + 1) * 512], in0=psf(b),
                                                                  in1=h[:, g, hf * 512:(hf + 1) * 512], op=ALU.add),
                                 reads=[PS.res[b], r_h[g]], writes=[r_h[g]])
                            PS.free(b)
                            yield
                        sq_stats(junk, r_junk, 2, g)

                A2s(0)
                run(A2(0, scale=False))
                run(Q2(0))
                A2s(1)
                run(A2(1, scale=False))
                for t in range(NT2):
                    nxt = Q2(t + 1) if t + 1 < NT2 else None
                    if t + 2 < NT2:
                        A2s(t + 2)
                    interleave((SH2(t), 3), (nxt, 2))
                    run(O2(t))
                    if t + 2 < NT2:
                        run(A2(t + 2, scale=False))
                finish_stats(2, 0, NSUB)
                S.barrier()

            s3 = ExitStack()
            with s3:
                gb, r_gb = gb3, r_gb3
                gbf = T(s3, "gbf", [128, D], F32)
                r_gbf = Res("gbf")
                S.dma("sp", gbf[:], g_fin_d.partition_broadcast(128), writes=[r_gbf])
                junk = T(s3, "junk3", [128, D], BF16)
                r_junk = Res("junk3")
                hsall = T(s3, "hs3", [128, 4, D], BF16)
                hs_bufs = [hsall[:, i, :] for i in range(4)]
                hs_res = [Res("hs3_%d" % i) for i in range(4)]
                yb = [hsall[:, 2 * j:2 * j + 2, :].rearrange("p a n -> p (a n)").bitcast(F32) for j in range(2)]
                r_yb = [Res("yb%d" % j) for j in range(2)]
                s3a = ExitStack()
                with s3a:
                    wg[1] = T(s3a, "wg1", [128, 8, GMAX * 128], BF16)
                    wu[1] = T(s3a, "wu1", [128, 8, GMAX * 128], BF16)
                    wd[1] = T(s3a, "wd1", [128, GMAX, D], BF16)
                    load_group(1)
                    hnT3 = T(s3a, "hnT3", [128, 8, TOK], BF16)
                    r_hnT3 = [Res("hnT3_%d" % g) for g in range(NSUB)]
                    sgb = [T(s3a, "sg%d" % i, [128, 512], F32) for i in range(2)]
                    r_sg = [Res("sg%d" % i) for i in range(2)]
                    hmid = [T(s3a, "hmid%d" % i, [128, GMAX, 512], BF16) for i in range(2)]
                    r_hmid = [Res("hmid%d" % i) for i in range(2)]
                    NT3 = TOK // 512

                    def A3s(t):
                        for s in range(4):
                            g = t * 4 + s
                            nt_scale(h[:, g, :], r_h[g], rstd[:, 2, g:g + 1], r_rstd[2][g], gb, r_gb, hs_bufs,
                                     hs_res, g)

                    def A3(t, scale=True):
                        for s in range(4):
                            g = t * 4 + s
                            norm_transpose(h[:, g, :], r_h[g], rstd[:, 2, g:g + 1], r_rstd[2][g], gb, r_gb, hs_bufs,
                                           hs_res, g, hnT3[:, :, g * 128:(g + 1) * 128], r_hnT3[g], scale=scale)
                            yield

                    steps = [(gi, t) for gi in range(len(FFN_GROUPS)) for t in range(NT3)]

                    def GU3(i):
                        gi, t = steps[i]
                        c0, G = FFN_GROUPS[gi]
                        bi = gi % 2
                        hm, rhm = hmid[i % 2], r_hmid[i % 2]
                        rds = [r_hnT3[t * 4 + s] for s in range(4)]
                        for c in range(G):
                            bg = PS.get()
                            mm_group(psf(bg), PS.res[bg],
                                     [(wg[bi][:, k, c * 128:(c + 1) * 128], hnT3[:, k, t * 512:(t + 1) * 512])
                                      for k in range(8)], [r_wg[bi]] + rds)
                            bu = PS.get()
                            mm_group(psf(bu), PS.res[bu],
                                     [(wu[bi][:, k, c * 128:(c + 1) * 128], hnT3[:, k, t * 512:(t + 1) * 512])
                                      for k in range(8)], [r_wu[bi]] + rds)
                            sg, rsg = sgb[c % 2], r_sg[c % 2]
                            S.op("act", lambda e: e.activation(out=sg[:], in_=psf(bg), func=AF.Silu),
                                 reads=[PS.res[bg]], writes=[rsg])
                            PS.free(bg)
                            S.op("dve", lambda e: e.tensor_tensor(out=hm[:, c, :], in0=psf(bu), in1=sg[:],
                                                                  op=ALU.mult), reads=[PS.res[bu], rsg], writes=[rhm])
                            PS.free(bu)
                            yield

                    def DN3(i):
                        gi, t = steps[i]
                        c0, G = FFN_GROUPS[gi]
                        bi = gi % 2
                        hm, rhm = hmid[i % 2], r_hmid[i % 2]
                        for s in range(4):
                            g = t * 4 + s
                            for hf in range(2):
                                b = PS.get()
                                mm_group(psf(b), PS.res[b],
                                         [(hm[:, c, s * 128:(s + 1) * 128], wd[bi][:, c, hf * 512:(hf + 1) * 512])
                                          for c in range(G)], [rhm, r_wd[bi]])
                                S.op("dve", lambda e: e.tensor_tensor(out=h[:, g, hf * 512:(hf + 1) * 512],
                                                                      in0=psf(b),
                                                                      in1=h[:, g, hf * 512:(hf + 1) * 512],
                                                                      op=ALU.add),
                                     reads=[PS.res[b], r_h[g]], writes=[r_h[g]])
                                PS.free(b)
                                yield
                            if gi == len(FFN_GROUPS) - 1:
                                sq_stats(junk, r_junk, 3, g)
                                finish_stats(3, g, g + 1)
                                y, ry = yb[g % 2], r_yb[g % 2]
                                if ry.w is None and not ry.r:
                                    for hr in (hs_res[2 * (g % 2)], hs_res[2 * (g % 2) + 1]):
                                        ry.w = hr.w
                                        for k_, v_ in hr.r.items():
                                            ry.r[k_] = max(ry.r.get(k_, 0), v_)
                                S.op("dve", lambda e: e.scalar_tensor_tensor(out=y, in0=h[:, g, :],
                                                                              scalar=rstd[:, 3, g:g + 1], in1=gbf[:],
                                                                              op0=ALU.mult, op1=ALU.mult),
                                     reads=[r_h[g], r_rstd[3][g], r_gbf], writes=[ry])
                                S.dma("sp", out_d[g * 128:(g + 1) * 128, :], y, reads=[ry])

                    A3s(0)
                    run(A3(0, scale=False))
                    A3s(1)
                    interleave((GU3(0), 1), (A3(1, scale=False), 1))
                    for i in range(len(steps)):
                        gi, t = steps[i]
                        nxt = GU3(i + 1) if i + 1 < len(steps) else None
                        extra = None
                        if gi == 0 and t + 2 < NT3:
                            A3s(t + 2)
                            extra = A3(t + 2, scale=False)
                        interleave((DN3(i), 1), (nxt, 1), (extra, 1))
                        if t == NT3 - 1 and gi + 2 < len(FFN_GROUPS):
                            load_group(gi + 2)
                    S.barrier()
    return nc


def make_in_maps(inputs):
    f = lambda a: np.ascontiguousarray(np.asarray(a, dtype=np.float32))
    x = f(inputs["x"])
    mem = f(inputs["mem"])
    B, SEQ, _ = x.shape
    per_b = SEQ // TOK
    assert B * per_b == NCORES
    b_in = f(inputs["b_in"])

    def pp(v):
        return np.ascontiguousarray(v.reshape(-1, 128).T)

    vec_base = np.concatenate([pp(b_in), pp(f(inputs["conv_b"])), pp(f(inputs["conv_ln_g"])),
                               pp(f(inputs["conv_ln_b"])), pp(f(inputs["gm_ln_g"])), pp(f(inputs["gm_ln_b"]))],
                              axis=1)
    conv_w = f(inputs["conv_w"])
    cw_pad = np.zeros((32, 512), np.float32)
    cw_pad[:KCONV] = conv_w
    cwp = np.ascontiguousarray(cw_pad.reshape(8, 4, 16, 32).transpose(1, 3, 2, 0).reshape(128, 16, 8))
    e32 = np.ascontiguousarray(np.tile(np.eye(32, dtype=np.float32), (4, 1)))
    w_s = f(inputs["gm_w_s"])
    wst = np.ascontiguousarray(w_s.transpose(2, 0, 1))
    sidx = np.arange(128)
    maskT = np.ascontiguousarray(
        np.broadcast_to((sidx[:, None] <= sidx[None, :]).astype(np.float32)[:, None, :], (128, 8, 128)))
    b_s = f(inputs["gm_b_s"])
    bs_rows = np.repeat(b_s, 64, axis=0).reshape(4, 128, 128).transpose(1, 0, 2)
    bs_rep = np.ascontiguousarray(bs_rows)
    ident = np.eye(128, dtype=np.float32)
    shared = {k: f(inputs[k]) for k in ("w_in", "w_out", "xa_wq", "xa_wkv", "xa_wo", "ffn_w_gate_up", "ffn_w_down",
                                        "norm_mix_g", "norm_xa_g", "norm_ffn_g", "final_norm_g", "mem_norm_g")}
    shared.update(bv_nat=np.ascontiguousarray(b_in[1536:2048]), gm_g_nat=f(inputs["gm_ln_g"]),
                  gm_b_nat=f(inputs["gm_ln_b"]))
    shared.update(cwp=cwp, E32=e32, wst=wst, maskT=maskT, bs_rep=bs_rep, ident=ident)
    in_maps = []
    for core in range(NCORES):
        b, part = divmod(core, per_b)
        t0 = part * TOK
        xs = x[b, t0:t0 + TOK]
        if t0 == 0:
            xh = np.zeros((128, D), np.float32)
            flag = 0.0
        else:
            xh = x[b, t0 - 128:t0]
            flag = 1.0
        vec = np.zeros((128, 40), np.float32)
        vec[:, :36] = vec_base
        vec[:, 36] = flag
        m = dict(shared)
        m.update(x=np.ascontiguousarray(xs), xh=np.ascontiguousarray(xh), mem=mem[b], vec=vec)
        in_maps.append(m)
    return in_maps, (B, SEQ, per_b)


def kernel(**inputs):
    in_maps, (B, SEQ, per_b) = make_in_maps(inputs)
    nc = build_nc()
    res = run_bass_kernel_spmd(nc, in_maps, core_ids=list(range(NCORES)))
    out = np.empty((B, SEQ, D), np.float32)
    for core in range(NCORES):
        b, part = divmod(core, per_b)
        out[b, part * TOK:(part + 1) * TOK] = res.results[core]["out"]
    return out
```
